# Optimizing a Trainium2 kernel written in Bass

```python
import math
import jax, jax.numpy as jnp
from jax import lax
import numpy as np

D_MODEL = 1024
BATCH = 8
SEQ = 2048
DEPTH = 1

RET_HEADS = 4
RET_DK = 64
RET_DV = 128
RET_CHUNK = 128
RET_ROPE_BASE = 10000.0
SWA_HEADS = 8
SWA_KV_HEADS = 2
SWA_HEAD_DIM = 64
SWA_GROUP = SWA_HEADS // SWA_KV_HEADS
WINDOW = 128
SWA_BLOCK = WINDOW
NUM_BUCKETS = 32
MAX_DISTANCE = 128
RET_QK = RET_HEADS * RET_DK
RET_V = RET_HEADS * RET_DV
SWA_Q = SWA_HEADS * SWA_HEAD_DIM
SWA_KV = SWA_KV_HEADS * SWA_HEAD_DIM
D_MIX = RET_V + SWA_Q
SPLIT_SIZES = (RET_QK, RET_QK, RET_V, RET_V, SWA_Q, SWA_KV, SWA_KV, SWA_Q)
D_IN = sum(SPLIT_SIZES)
NORM_EPS = 1e-6
GN_EPS = 1e-5
NEG_INF = -1e30

kernel_name = "hybrid_retention_swa_sink_layer"


def rms_norm(x, w, eps=NORM_EPS):
    xf = x.astype(jnp.float32)
    y = xf * lax.rsqrt(jnp.mean(xf * xf, axis=-1, keepdims=True) + eps)
    return y * w.astype(jnp.float32)


def rotary(t, base):
    S, d = t.shape[1], t.shape[-1]
    half = d // 2
    inv_freq = base ** (-jnp.arange(half, dtype=jnp.float32) / half)
    ang = jnp.arange(S, dtype=jnp.float32)[:, None] * inv_freq[None, :]
    cos = jnp.cos(ang)[None, :, None, :]
    sin = jnp.sin(ang)[None, :, None, :]
    t1, t2 = t[..., :half], t[..., half:]
    return jnp.concatenate([t1 * cos - t2 * sin, t1 * sin + t2 * cos], axis=-1)


def retention(q, k, v):
    B, S = q.shape[0], q.shape[1]
    N, C, H = S // RET_CHUNK, RET_CHUNK, RET_HEADS
    k = k * (RET_DK ** -0.5)

    def chunks(t):
        return t.reshape(B, N, C, H, t.shape[-1]).transpose(0, 3, 1, 2, 4)

    qc, kc, vc = chunks(q), chunks(k), chunks(v)
    gamma = 1.0 - jnp.exp2(-5.0 - jnp.arange(H, dtype=jnp.float32))
    log_g = jnp.log(gamma)
    i = jnp.arange(C, dtype=jnp.float32)
    diff = i[:, None] - i[None, :]
    decay = jnp.where(diff >= 0, jnp.exp(log_g[:, None, None] * jnp.maximum(diff, 0.0)), 0.0)
    scores = jnp.einsum('bhnid,bhnjd->bhnij', qc, kc) * decay[None, :, None]
    intra = jnp.einsum('bhnij,bhnje->bhnie', scores, vc)
    zeta = jnp.exp(log_g[:, None] * (C - 1.0 - i))
    kv = jnp.einsum('bhnjd,bhnje->nbhde', kc * zeta[None, :, None, :, None], vc)
    chunk_decay = jnp.exp(log_g * C)[None, :, None, None]

    def step(state, kv_n):
        return chunk_decay * state + kv_n, state

    _, prev = lax.scan(step, jnp.zeros((B, H, RET_DK, RET_DV), jnp.float32), kv)
    xi = jnp.exp(log_g[:, None] * (i + 1.0))
    cross = jnp.einsum('bhnid,nbhde->bhnie', qc * xi[None, :, None, :, None], prev)
    return (intra + cross).transpose(0, 2, 3, 1, 4).reshape(B, S, H, RET_DV)


def t5_bucket(n):
    max_exact = NUM_BUCKETS // 2
    nf = jnp.maximum(n, 1).astype(jnp.float32)
    large = max_exact + (jnp.log(nf / max_exact) / math.log(MAX_DISTANCE / max_exact)
                         * (NUM_BUCKETS - max_exact)).astype(jnp.int32)
    large = jnp.minimum(large, NUM_BUCKETS - 1)
    return jnp.where(n < max_exact, n, large)


def sliding_window_attention(q, k, v, q_norm_w, k_norm_w, sinks, rel_bias):
    B, S = q.shape[0], q.shape[1]
    N, C = S // SWA_BLOCK, SWA_BLOCK
    q = rms_norm(q, q_norm_w)
    k = rms_norm(k, k_norm_w)
    qb = q.reshape(B, N, C, SWA_KV_HEADS, SWA_GROUP, SWA_HEAD_DIM)

    def band(t):
        tb = t.reshape(B, N, C, SWA_KV_HEADS, SWA_HEAD_DIM)
        prev = jnp.pad(tb, ((0, 0), (1, 0), (0, 0), (0, 0), (0, 0)))[:, :-1]
        return jnp.concatenate([prev, tb], axis=2)

    kband, vband = band(k), band(v)
    logits = jnp.einsum('bnqhgd,bnkhd->bhgnqk', qb, kband) * (SWA_HEAD_DIM ** -0.5)
    qi = jnp.arange(C)[:, None]
    kj = jnp.arange(2 * C)[None, :]
    dist = qi + C - kj
    bucket = t5_bucket(jnp.maximum(dist, 0))
    bias = rel_bias[bucket].astype(jnp.float32).transpose(2, 0, 1)
    bias = bias.reshape(SWA_KV_HEADS, SWA_GROUP, 1, C, 2 * C)
    key_pos = jnp.arange(N)[:, None, None] * C - C + kj[None]
    mask = (dist[None] >= 0) & (dist[None] < WINDOW) & (key_pos >= 0)
    logits = jnp.where(mask, logits + bias, NEG_INF)
    sink = jnp.broadcast_to(sinks.astype(jnp.float32).reshape(SWA_KV_HEADS, SWA_GROUP, 1, 1, 1),
                            logits.shape[:-1] + (1,))
    probs = jax.nn.softmax(jnp.concatenate([logits, sink], axis=-1), axis=-1)[..., :-1]
    out = jnp.einsum('bhgnqk,bnkhd->bnqhgd', probs, vband)
    return out.reshape(B, S, SWA_Q)


def setup_inputs(seed: int = 0) -> dict:
    key = jax.random.key(seed)
    ks = jax.random.split(key, 9)
    f32 = jnp.float32
    x = jax.random.normal(ks[0], (BATCH, SEQ, D_MODEL), f32)
    norm_w = 1.0 + 0.02 * jax.random.normal(ks[1], (D_MODEL,), f32)
    w_in = jax.random.normal(ks[2], (D_MODEL, D_IN), f32) * D_MODEL ** -0.5
    ret_norm_w = 1.0 + 0.02 * jax.random.normal(ks[3], (RET_V,), f32)
    q_norm_w = 1.0 + 0.02 * jax.random.normal(ks[4], (SWA_HEAD_DIM,), f32)
    k_norm_w = 1.0 + 0.02 * jax.random.normal(ks[5], (SWA_HEAD_DIM,), f32)
    sinks = 0.5 * jax.random.normal(ks[6], (SWA_HEADS,), f32)
    rel_bias = 0.1 * jax.random.normal(ks[7], (NUM_BUCKETS, SWA_HEADS), f32)
    w_out = jax.random.normal(ks[8], (D_MIX, D_MODEL), f32) * D_MIX ** -0.5
    return {"x": x, "norm_w": norm_w, "w_in": w_in, "ret_norm_w": ret_norm_w,
            "q_norm_w": q_norm_w, "k_norm_w": k_norm_w, "sinks": sinks,
            "rel_bias": rel_bias, "w_out": w_out}


def reference(x, norm_w, w_in, ret_norm_w, q_norm_w, k_norm_w, sinks, rel_bias, w_out):
    B, S = x.shape[0], x.shape[1]
    offsets = []
    acc = 0
    for sz in SPLIT_SIZES[:-1]:
        acc += sz
        offsets.append(acc)
    for _ in range(DEPTH):
        h = rms_norm(x, norm_w).astype(x.dtype)
        proj = jnp.einsum('bsd,de->bse', h, w_in).astype(jnp.float32)
        rq, rk, rv, rg, sq, sk, sv, sg = jnp.split(proj, offsets, axis=-1)
        rq = rotary(rq.reshape(B, S, RET_HEADS, RET_DK), RET_ROPE_BASE)
        rk = rotary(rk.reshape(B, S, RET_HEADS, RET_DK), RET_ROPE_BASE)
        ro = retention(rq, rk, rv.reshape(B, S, RET_HEADS, RET_DV))
        mu = jnp.mean(ro, axis=-1, keepdims=True)
        var = jnp.mean(jnp.square(ro - mu), axis=-1, keepdims=True)
        ro = ((ro - mu) * lax.rsqrt(var + GN_EPS)).reshape(B, S, RET_V) * ret_norm_w.astype(jnp.float32)
        ro = ro * jax.nn.silu(rg)
        so = sliding_window_attention(
            sq.reshape(B, S, SWA_HEADS, SWA_HEAD_DIM),
            sk.reshape(B, S, SWA_KV_HEADS, SWA_HEAD_DIM),
            sv.reshape(B, S, SWA_KV_HEADS, SWA_HEAD_DIM),
            q_norm_w, k_norm_w, sinks, rel_bias)
        so = so * jax.nn.silu(sg)
        mixed = jnp.concatenate([ro, so], axis=-1).astype(x.dtype)
        x = x + jnp.einsum('bse,ed->bsd', mixed, w_out)
    return x
```

```python
import math
import os
from contextlib import ExitStack

import numpy as np

import concourse.bass as bass
import concourse.mybir as mybir
from concourse.bass_utils import run_bass_kernel_spmd

F32 = mybir.dt.float32
BF16 = mybir.dt.bfloat16
AF = mybir.ActivationFunctionType
ALU = mybir.AluOpType
AX = mybir.AxisListType

D_MODEL = 1024
SEQ = 2048
NT = SEQ // 128
D_IN = 2816
N_CORES = 8
NORM_EPS = 1e-6
GN_EPS = 1e-5

P_QK = (0, 512)
P_V = (512, 512)
P_RG = (1024, 512)
P_SQ = (1536, 512)
P_SKV = (2048, 256)
P_SG = (2304, 512)


class PG:
    def __init__(self, kind, key, default):
        self.kind, self.key, self.default, self.t = kind, key, default, None

    def __getitem__(self, idx):
        return LazyAP(self, lambda t, idx=idx: t[idx])


class LazyAP:
    def __init__(self, g, fn):
        self.g, self.fn = g, fn

    def ap(self):
        return self.fn(self.g.t if self.g.t is not None else self.g.default)

    @property
    def shape(self):
        return self.fn(self.g.default).shape

    @property
    def dtype(self):
        return self.fn(self.g.default).dtype

    def __getitem__(self, idx):
        return LazyAP(self.g, lambda t, f=self.fn, idx=idx: f(t)[idx])

    def rearrange(self, pat, **kw):
        return LazyAP(self.g, lambda t, f=self.fn: f(t).rearrange(pat, **kw))


def R(a):
    return a.ap() if isinstance(a, LazyAP) else a


class Sched:
    SEM_LAT = 0.12
    TBL_SWITCH = 1.3
    WINDOW = int(os.environ.get("K_WINDOW", "360"))

    def __init__(self):
        self.ops = []
        self.lw = {}
        self.rd = {}
        self.groups = {}
        self.pgs = {}
        self.banks = {}

    def op(self, eng, fn, r=(), w=(), dma=None, cost=0.1, lat=0.0):
        idx = len(self.ops)
        deps = {}
        for k in r:
            if k in self.lw:
                deps[self.lw[k]] = "raw"
            if k[:2] in ("pf", "tp"):
                for x in self.rd.get(k, ()):
                    if self.ops[x]["eng"] != eng:
                        deps.setdefault(x, "rr")
        for k in w:
            if k in self.lw:
                deps.setdefault(self.lw[k], "waw")
            for x in self.rd.get(k, ()):
                deps.setdefault(x, "war")
        for k in r:
            self.rd.setdefault(k, []).append(idx)
        for k in w:
            self.lw[k] = idx
            self.rd[k] = []
        self.ops.append(dict(eng=eng, fn=fn, alldeps=deps, dma=dma, cost=cost, lat=lat, signal=dma is not None))
        for k in set(list(r) + list(w)):
            if "#" in k:
                self.groups.setdefault(k, []).append(idx)
        return idx

    def _needs_sem(self, P, C, kind):
        if P["dma"] is not None:
            return True
        if P["eng"] != C["eng"]:
            return True
        if C["dma"] is not None:
            return True
        if C["eng"] == "pe":
            return False
        return kind == "raw"

    def evaluate(self, order):
        ops = self.ops
        ptr = {e: 0 for e in order}
        eng_free = {e: 0.0 for e in order}
        fin = {}
        endi = {}
        dma_free = 0.0
        cur_tbl = "exp"
        left = sum(len(v) for v in order.values())
        while left:
            progressed = False
            for e, lst in order.items():
                while ptr[e] < len(lst):
                    i = lst[ptr[e]]
                    o = ops[i]
                    if any(d not in fin for d in o["alldeps"]):
                        break
                    st = eng_free[e]
                    for d, kind in o["alldeps"].items():
                        P = ops[d]
                        if P["eng"] == e and not self._needs_sem(P, o, kind):
                            st = max(st, endi[d])
                        else:
                            st = max(st, fin[d] + self.SEM_LAT)
                    c = o["true_cost"]
                    if o["dma"] is not None:
                        issue = 1.0 if e == "pool" else 0.06
                        eng_free[e] = st + issue
                        t0 = max(st + issue, dma_free)
                        dma_free = t0 + c
                        fin[i] = dma_free + 2.0
                        endi[i] = st + issue
                    else:
                        if e == "act" and o.get("tbl") and o["tbl"] != cur_tbl:
                            c += self.TBL_SWITCH
                            cur_tbl = o["tbl"]
                        eng_free[e] = st + c
                        endi[i] = st + c
                        fin[i] = st + c + o["lat"]
                    ptr[e] += 1
                    left -= 1
                    progressed = True
            assert progressed, "evaluate: deadlock"
        return max(fin.values())

    def schedule(self):
        ops = self.ops
        n = len(ops)
        import random as _random
        _rng = _random.Random(int(os.environ.get("K_SEED", "0")))
        _nz = float(os.environ.get("K_NOISE", "0"))
        noise = [(_rng.uniform(0.0, _nz) if _nz > 0 else 0.0) for _ in range(n)]
        pes = float(os.environ.get("K_PESCALE", "1.1"))
        dvs = float(os.environ.get("K_DVESCALE", "1.0"))
        for o in ops:
            o.setdefault("true_cost", o["cost"])
            if o["eng"] == "pe":
                o["cost"] = o["true_cost"] * pes
            elif o["eng"] == "dve":
                o["cost"] = o["true_cost"] * dvs
        self.groups_of = {}
        for k, lst in self.groups.items():
            for i in lst:
                self.groups_of.setdefault(i, []).append(k)
        succ = [[] for _ in range(n)]
        indeg = [0] * n
        for i, o in enumerate(ops):
            for d in o["alldeps"]:
                succ[d].append(i)
                indeg[i] += 1
        finish = [0.0] * n
        end_issue = [0.0] * n
        done = [False] * n

        def dep_ready(o, e, d, kind):
            P = ops[d]
            if P["eng"] == e and not self._needs_sem(P, o, kind):
                return end_issue[d]
            return finish[d] + self.SEM_LAT
        eng_free = {}
        ready = {}
        for i, o in enumerate(ops):
            eng_free.setdefault(o["eng"], 0.0)
            ready.setdefault(o["eng"], [])
            if indeg[i] == 0:
                ready[o["eng"]].append(i)
        order = {e: [] for e in ready}
        nsched = 0
        low = 0
        dma_free = 0.0
        cur_tbl = ["exp"]
        first_of = {lst[0]: k for k, lst in self.groups.items()}
        remaining = {k: len(lst) for k, lst in self.groups.items()}
        bind_q = {}
        for k in sorted(self.groups, key=lambda k: self.groups[k][0]):
            bind_q.setdefault(self.pgs[k].kind, []).append(k)
        bind_ptr = {kind: 0 for kind in bind_q}
        bank_state = {kind: [dict(group=None, free=0.0) for _ in tl] for kind, tl in self.banks.items()}

        def bank_for(i):
            k = first_of.get(i)
            if k is None:
                return None
            kind = self.pgs[k].kind
            if bind_q[kind][bind_ptr[kind]] != k:
                return (kind, -1, 0.0)
            best_b = -1
            for b, bs in enumerate(bank_state[kind]):
                if bs["group"] is None or remaining[bs["group"]] == 0:
                    if best_b < 0 or bs["free"] < bank_state[kind][best_b]["free"]:
                        best_b = b
            return (kind, best_b, bank_state[kind][best_b]["free"] if best_b >= 0 else 0.0)
        while nsched < n:
            while low < n and done[low]:
                low += 1
            best = None
            for e, lst in ready.items():
                for i in lst:
                    if i > low + self.WINDOW:
                        continue
                    o = ops[i]
                    st = eng_free[e]
                    bk = bank_for(i)
                    if bk is not None:
                        if bk[1] < 0:
                            continue
                        st = max(st, bk[2] + self.SEM_LAT)
                    for d, kind in o["alldeps"].items():
                        f = dep_ready(o, e, d, kind)
                        if f > st:
                            st = f
                    pen = self.TBL_SWITCH if (o.get("tbl") and o["tbl"] != cur_tbl[0]) else 0.0
                    key = (st + pen + noise[i], i)
                    if best is None or key < best[0]:
                        best = (key, e, i)
            if best is None:
                cand = sorted((i, e) for e, lst in ready.items() for i in lst)
                for i, e in cand:
                    bk = bank_for(i)
                    if bk is not None and bk[1] < 0:
                        continue
                    st = max([eng_free[e]] + [dep_ready(ops[i], e, d, kd) for d, kd in ops[i]["alldeps"].items()]
                             + ([bk[2] + self.SEM_LAT] if bk is not None else []))
                    best = ((st, i), e, i)
                    break
                assert best is not None, "scheduler deadlock (PSUM banks)"
            (st, i), e, _ = best
            o = ops[i]
            tsw = 0.0
            if o.get("tbl") and o["tbl"] != cur_tbl[0]:
                st -= self.TBL_SWITCH if best[0][0] - self.TBL_SWITCH >= eng_free[e] - 1e-9 else 0.0
                st = max(st, eng_free[e])
                tsw = self.TBL_SWITCH
                cur_tbl[0] = o["tbl"]
            ready[e].remove(i)
            bk = bank_for(i)
            if bk is not None:
                kind, b, _f = bk
                k = first_of[i]
                prevg = bank_state[kind][b]["group"]
                if prevg is not None:
                    for j in self.groups[prevg]:
                        o["alldeps"].setdefault(j, "war")
                bank_state[kind][b]["group"] = k
                self.pgs[k].t = self.banks[kind][b]
                bind_ptr[kind] += 1
            bind = ("eng", order[e][-1] if order[e] else -1)
            if not (order[e] and abs(eng_free[e] - st) < 1e-9):
                for d, kind in o["alldeps"].items():
                    f = dep_ready(o, e, d, kind)
                    if abs(f - st) < 1e-9:
                        bind = ("dep", d)
            o["bind"], o["st"] = bind, st
            if o["dma"] is not None:
                issue = 1.0 if e == "pool" else 0.06
                eng_free[e] = st + issue
                t0 = max(st + issue, dma_free)
                dma_free = t0 + o["cost"]
                finish[i] = dma_free + 2.0
                end_issue[i] = st + issue
            else:
                eng_free[e] = st + o["cost"] + tsw
                end_issue[i] = st + o["cost"] + tsw
                finish[i] = st + o["cost"] + tsw + o["lat"]
            done[i] = True
            order[e].append(i)
            nsched += 1
            for k in self.groups_of.get(i, ()):
                remaining[k] -= 1
                if remaining[k] == 0:
                    kind = self.pgs[k].kind
                    for bs in bank_state[kind]:
                        if bs["group"] == k:
                            bs["free"] = max(finish[j] for j in self.groups[k])
            for j in succ[i]:
                indeg[j] -= 1
                if indeg[j] == 0:
                    ready[ops[j]["eng"]].append(j)
        self.makespan = max(finish)
        self.finish = finish
        self.true_makespan = self.evaluate(order)
        return order

    def emit(self, nc, es):
        ops = self.ops
        order = self.schedule()
        for i, o in enumerate(ops):
            o["deps"] = []
            for d, kind in o["alldeps"].items():
                if self._needs_sem(ops[d], o, kind):
                    o["deps"].append(d)
                    ops[d]["signal"] = True
        eng_count = {}
        dma_count = {}
        for e, lst in order.items():
            for i in lst:
                o = ops[i]
                if not o["signal"]:
                    continue
                if o["dma"] is not None:
                    k = "D_" + o["dma"]
                    dma_count[k] = dma_count.get(k, 0) + 16
                    o["sem"], o["val"] = k, dma_count[k]
                else:
                    k = "E_" + o["eng"]
                    eng_count[k] = eng_count.get(k, 0) + 1
                    o["sem"], o["val"] = k, eng_count[k]
        semh = {}
        for k in list(eng_count) + list(dma_count):
            semh[k] = es.enter_context(nc.semaphore(k))

        def mk(engname):
            def body(engine):
                waited = {}
                for i in order.get(engname, []):
                    o = ops[i]
                    for d in sorted(o["deps"], key=lambda d: (ops[d]["sem"], ops[d]["val"])):
                        P = ops[d]
                        s, v = P["sem"], P["val"]
                        if waited.get(s, 0) >= v:
                            continue
                        engine.wait_ge(semh[s], v)
                        waited[s] = v
                    if o["fn"] is not None:
                        ins = o["fn"](engine)
                        if o["signal"]:
                            ins.then_inc(semh[o["sem"]], 16 if o["dma"] is not None else 1)
            return body

        with nc.Block() as block:
            block.tensor(mk("pe"))
            block.scalar(mk("act"))
            block.vector(mk("dve"))
            block.gpsimd(mk("pool"))
            block.sync(mk("sp"))


def _t5_bucket_np(n):
    n = np.asarray(n, dtype=np.int32)
    max_exact = 16
    nf = np.maximum(n, 1).astype(np.float32)
    large = max_exact + (np.log(nf / np.float32(max_exact)) / np.float32(math.log(128 / max_exact))
                         * np.float32(32 - max_exact)).astype(np.int32)
    large = np.minimum(large, 31)
    return np.where(n < max_exact, n, large)


def _consts():
    c = {}
    c["ident"] = np.eye(128, dtype=np.float32)
    jj = np.arange(128)[:, None]
    ii = np.arange(128)[None, :]
    c["maskc"] = (ii >= jj).astype(np.float32)
    c["maskp"] = (ii < jj).astype(np.float32)
    u = np.arange(256) % 128
    bk = _t5_bucket_np(u)
    ohb = np.zeros((32, 256), np.float32)
    ohb[bk, np.arange(256)] = 1.0
    c["ohb"] = ohb
    half = 32
    inv_freq = 10000.0 ** (-np.arange(half, dtype=np.float64) / half)
    pos = np.arange(SEQ, dtype=np.float64)
    ang = pos[:, None] * inv_freq[None, :]
    cos = np.cos(ang)
    sin = np.sin(ang)
    h = np.arange(4, dtype=np.float64)
    gamma = 1.0 - np.exp2(-5.0 - h)
    r = (np.arange(SEQ) % 128 + 1).astype(np.float64)
    sq = gamma[None, :] ** r[:, None]
    sk = gamma[None, :] ** (-r[:, None]) * (64.0 ** -0.5)
    s = np.concatenate([sq, sk], axis=1)
    c["cs"] = (cos[:, None, :] * s[:, :, None]).reshape(SEQ, 256).astype(np.float32)
    c["sn"] = (sin[:, None, :] * s[:, :, None]).reshape(SEQ, 256).astype(np.float32)
    gC = np.zeros((128, 2), np.float64)
    for pair in range(2):
        gC[:64, pair] = gamma[2 * pair] ** 128
        gC[64:, pair] = gamma[2 * pair + 1] ** 128
    c["gC"] = gC.astype(np.float32)
    return c


def build_nc(dbg_tile=None, NT=NT):
    SEQ = NT * 128
    nc = bass.Bass("TRN2", target_bir_lowering=False)
    S = Sched()

    def din(name, shape):
        return nc.dram_tensor(name, list(shape), F32, kind="ExternalInput").ap()

    x_d = din("x", [SEQ, D_MODEL])
    normw_d = din("norm_w", [D_MODEL])
    win_d = din("w_in", [D_MODEL, D_IN])
    retw_d = din("ret_norm_w", [512])
    qw_d = din("q_norm_w", [64])
    kw_d = din("k_norm_w", [64])
    sinks_d = din("sinks", [8])
    relb_d = din("rel_bias", [32, 8])
    wout_d = din("w_out", [D_MODEL, D_MODEL])
    ident_d = din("ident", [128, 128])
    maskc_d = din("maskc", [128, 128])
    maskp_d = din("maskp", [128, 128])
    ohb_d = din("ohb", [32, 256])
    cs_d = din("cs", [2048, 256])
    sn_d = din("sn", [2048, 256])
    gC_d = din("gC", [128, 2])
    out_d = nc.dram_tensor("out", [SEQ, D_MODEL], F32, kind="ExternalOutput").ap()
    SCR_H = 128 * 257
    scr_d = nc.dram_tensor("scr", [8 * SCR_H], F32, kind="Internal").ap()
    dbg_outs = {}

    with ExitStack() as es:
        def sb(name, shape, dt=F32):
            return es.enter_context(nc.sbuf_tensor("s_" + name, list(shape), dt))

        def psum(name, shape, dt=F32):
            return es.enter_context(nc.psum_tensor("p_" + name, list(shape), dt))

        def bcast_rows(d_ap, n):
            return bass.AP(d_ap.tensor, d_ap.offset, [[0, 128], [1, n]])

        w_in_bf = sb("w_in_bf", [128, 8, D_IN], BF16)
        w_out_bf = sb("w_out_bf", [128, 8, D_MODEL], BF16)
        ident_f = sb("ident_f", [128, 128])
        ident = sb("ident_b", [128, 128], BF16)
        maskc = sb("maskc", [128, 128])
        maskp = sb("maskp", [128, 128])
        negc = sb("negc", [128, 128])
        negp = sb("negp", [128, 128])
        ohb = sb("ohb", [32, 256])
        rb32 = sb("rb32", [32, 8])
        ones32 = sb("ones32", [32, 128])
        gC = sb("gC", [128, 2])
        normw_col = sb("normw_col", [128, 8])
        retw_bc = sb("retw_bc", [128, 512])
        qw_bc = sb("qw_bc", [128, 64])
        kw_bc = sb("kw_bc", [128, 64])
        kwq_bc = sb("kwq_bc", [128, 128])
        qkw = sb("qkw", [128, 64])
        sinks_bc = sb("sinks_bc", [128, 8])
        esink = sb("esink", [128, 8])
        mtmp = sb("mtmp", [128, 4])
        negM = sb("negM", [128, 1])
        eps6 = sb("eps6", [128, 1])
        eps5 = sb("eps5", [128, 1])
        one1 = sb("one1", [128, 1])
        gtmp = sb("gtmp", [128, 512])
        NWFOLD_ = int(os.environ.get("K_NWFOLD", "0"))
        if NWFOLD_:
            nw_bc = sb("nw_bc", [128, D_MODEL])
            xw = sb("xw", [128, D_MODEL])
        rhsB = sb("rhsB", [32, 8, 256])
        W_sb = sb("W_sb", [128, 8, 256])
        EBc = sb("EBc", [128, 8, 128])
        EBp = sb("EBp", [128, 8, 128])
        OSPLIT = int(os.environ.get("K_OSPLIT", "1"))
        B4S = int(os.environ.get("K_B4S", "0"))
        B4E = os.environ.get("K_B4E", "pool")
        B2S = int(os.environ.get("K_B2S", "1"))
        B2E = os.environ.get("K_B2E", "pool")
        BM = os.environ.get("K_BM", "mmam")
        if "m" in BM:
            EMc = sb("EMc", [128, 8, 128])
            EMp = sb("EMp", [128, 8, 128])

        NXS = 4
        xs = [sb(f"xs{i}", [128, D_MODEL]) for i in range(NXS)]
        cs_t = [sb(f"cs{i}", [128, 256]) for i in range(3)]
        sn_t = [sb(f"sn{i}", [128, 256]) for i in range(3)]
        ss = [sb(f"ss{i}", [128, 4]) for i in range(2)]
        xn = [sb(f"xn{i}", [128, D_MODEL], BF16) for i in range(2)]
        xT = [sb(f"xT{i}", [128, 8, 128], BF16) for i in range(2)]
        t1 = sb("t1", [128, 512])
        t2 = sb("t2", [128, 512])
        qkr = [sb(f"qkr{i}", [128, 512], BF16) for i in range(2)]
        qkT = [sb(f"qkT{i}", [128, 4, 128], BF16) for i in range(2)]
        vb = [sb(f"vb{i}", [128, 512], BF16) for i in range(2)]
        gate_r = [sb(f"gate_r{i}", [128, 512]) for i in range(2)]
        sqj = sb("sqj", [128, 640])
        ssq = [sb(f"ssq{i}", [128, 32]) for i in range(2)]
        qn = [sb(f"qn{i}", [128, 512], BF16) for i in range(2)]
        qnT = [sb(f"qnT{i}", [128, 512], BF16) for i in range(2)]
        kn = [sb(f"kn{i}", [128, 128], BF16) for i in range(2)]
        knT = [sb(f"knT{i}", [128, 128], BF16) for i in range(3)]
        vext = [sb(f"vext{i}", [128, 2, 65], BF16) for i in range(3)]
        gate_s = [sb(f"gate_s{i}", [128, 512]) for i in range(2)]
        sT = [sb(f"sT{i}", [128, 512], BF16) for i in range(2)]
        Tst = sb("Tst", [128, 2, 256])
        Tb = [sb(f"Tb{i}", [128, 2, 256], BF16) for i in range(2)]
        bst = sb("bst", [128, 4, 6])
        bmv = [sb(f"bmv{i}", [128, 16]) for i in range(2)]
        ybuf = sb("ybuf", [128, 512])
        mixed = [sb(f"mixed{i}", [128, D_MODEL], BF16) for i in range(2)]
        Praw = [sb(f"Praw{i}", [128, 512]) for i in range(4)]
        Pbf = [sb(f"Pbf{i}", [128, 512], BF16) for i in range(4)]
        den = [sb(f"den{i}", [128, 16]) for i in range(2)]
        gs2 = sb("gs2", [128, 512])
        mT = sb("mT", [128, 8, 128], BF16)
        ot = [sb(f"ot{i}", [128, D_MODEL]) for i in range(2)]

        tp = [psum(f"tp{i}", [128, 1024], BF16) for i in range(2)]
        pf = [psum(f"pf{i}", [128, 512], F32) for i in range(6)]
        cnt = {"tp": 0, "pf": 0}

        S.banks = {"tp": tp, "pf": pf}
        gcount = [0]

        def next_tp(cls=None):
            gcount[0] += 1
            g = PG("tp", f"tp#{gcount[0]}", tp[0])
            S.pgs[g.key] = g
            return g, g.key

        def next_pf(cls=None):
            gcount[0] += 1
            g = PG("pf", f"pf#{gcount[0]}", pf[0])
            S.pgs[g.key] = g
            return g, g.key

        def fsz(ap):
            n = 1
            for d in ap.shape[1:]:
                n *= d
            return n

        def is_ps(ap):
            return isinstance(ap, LazyAP) or "PSum" in type(ap.tensor).__name__

        def dma(eng, out, in_, r, w, key):
            nbytes = fsz(out) * out.shape[0] * 4
            S.op(eng, lambda e, out=out, in_=in_: e.dma_start(out=out, in_=in_), r=r, w=w, dma=key,
                 cost=nbytes / 290e3)

        def mm(out, lhsT, rhs, start, stop, r, w):
            n = fsz(rhs)
            c = 0.03 if n <= 65 else 0.06 if n <= 128 else 0.11 if n <= 256 else 0.22
            if rhs.dtype == F32:
                c *= 4
            S.op("pe", lambda e, out=out, lhsT=lhsT, rhs=rhs, start=start, stop=stop:
                 e.matmul(R(out), lhsT, rhs, start=start, stop=stop), r=r, w=w, cost=c, lat=0.15)

        def tr(out, in_, r, w):
            S.op("pe", lambda e, out=out, in_=in_: e.transpose(R(out), in_, ident[:, :]), r=list(r) + ["ident"], w=w,
                 cost=0.06, lat=0.15)

        def act(out, in_, func, r, w, bias=None, scale=None, accum=None):
            kw = {}
            if bias is not None:
                kw["bias"] = bias
            if scale is not None:
                kw["scale"] = scale
            if accum is not None:
                kw["accum_out"] = accum
            c = 0.2 + fsz(in_) / 1300.0 + (0.1 if accum is not None else 0.0)
            i_ = S.op("act", lambda e, out=out, in_=in_, func=func, kw=kw: e.activation(R(out), R(in_), func, **kw), r=r, w=w,
                      cost=c, lat=0.1)
            S.ops[i_]["tbl"] = "silu" if func == AF.Silu else ("exp" if func in (AF.Exp, AF.Ln) else None)

        def ew_cost(eng, out, ins):
            n = fsz(out)
            if eng == "pool":
                return 0.12 + n / (440.0 if len(ins) > 1 else 950.0)
            nsb = sum(1 for a in ins if not is_ps(a) and a.dtype == F32)
            return 0.07 + n / (425.0 if (len(ins) > 1 and nsb > 1) else 850.0)

        def tt(eng, out, in0, in1, op, r, w):
            S.op(eng, lambda e, out=out, in0=in0, in1=in1, op=op: e.tensor_tensor(R(out), R(in0), R(in1), op), r=r, w=w,
                 cost=ew_cost(eng, out, [in0, in1]), lat=0.1)

        def ts(eng, out, in0, s1, s2, op0, op1, r, w):
            S.op(eng, lambda e, out=out, in0=in0, s1=s1, s2=s2, op0=op0, op1=op1:
                 e.tensor_scalar(R(out), R(in0), s1, s2, op0, op1), r=r, w=w, cost=ew_cost(eng, out, [in0]), lat=0.1)

        def stt(out, in0, sc, in1, op0, op1, r, w):
            S.op("dve", lambda e, out=out, in0=in0, sc=sc, in1=in1, op0=op0, op1=op1:
                 e.scalar_tensor_tensor(R(out), R(in0), sc, R(in1), op0, op1), r=r, w=w,
                 cost=ew_cost("dve", out, [in0, in1]), lat=0.1)

        def cp(eng, out, in_, r, w):
            c = ew_cost(eng, out, [in_])
            if eng == "dve" and out.dtype == BF16 and in_.dtype == BF16:
                c = 0.1 + fsz(out) / 1750.0
            S.op(eng, lambda e, out=out, in_=in_: e.tensor_copy(R(out), R(in_)), r=r, w=w, cost=c, lat=0.1)

        CPE = os.environ.get("K_CPE", "aaad")
        XSPLIT = int(os.environ.get("K_XSPLIT", "2"))
        MSPLIT = int(os.environ.get("K_MSPLIT", "8"))
        NWFOLD = int(os.environ.get("K_NWFOLD", "0"))

        def evac(which, out, in_, r, w):
            if which == "a":
                act(out, in_, AF.Copy, r=r, w=w)
            else:
                cp("dve", out, in_, r=r, w=w)

        def memset(eng, ap, val, w):
            S.op(eng, lambda e, ap=ap, val=val: e.memset(ap, val), r=(), w=w, cost=0.1 + fsz(ap) / 2000.0)

        REL = os.environ.get("K_REL", "B3")

        def release(seg, t):
            if REL == seg:
                i_ = len(S.ops) - 1
                k = f"rel{t}"
                S.lw[k] = i_
                S.rd[k] = []

        def rsqrt_chain(src, ln_out, rs_out, scale, eps_tile, key):
            act(ln_out, src, AF.Ln, r=[key, "eps"], w=[key], bias=eps_tile[:, 0:1], scale=scale)
            act(rs_out, ln_out, AF.Exp, r=[key], w=[key], scale=-0.5)

        def load_tile(t):
            dma("sp", xs[t % NXS][:, :], x_d[t * 128:(t + 1) * 128, :], r=(), w=[f"xs{t % NXS}"], key=f"xs{t % NXS}")

        def load_rot(t):
            dma("sp", cs_t[t % 3][:, :], cs_d[t * 128:(t + 1) * 128, :], r=(), w=[f"cs{t % 3}"], key=f"cs{t % 3}")
            dma("sp", sn_t[t % 3][:, :], sn_d[t * 128:(t + 1) * 128, :], r=(), w=[f"sn{t % 3}"], key=f"sn{t % 3}")

        small = [
            (ident_f[:, :], ident_d, "ident_f"), (maskc[:, :], maskc_d, "maskc"), (maskp[:, :], maskp_d, "maskp"),
            (ohb[:, :], ohb_d, "ohb"), (rb32[:, :], relb_d, "rb32"), (gC[:, :], gC_d, "gC"),
            (normw_col[:, :], normw_d.rearrange("(c p) -> p c", p=128), "normw_col"),
            (retw_bc[:, :], bcast_rows(retw_d, 512), "retw_bc"),
            (qw_bc[:, :], bcast_rows(qw_d, 64), "qw_bc"), (kw_bc[:, :], bcast_rows(kw_d, 64), "kw_bc"),
            (sinks_bc[:, :], bcast_rows(sinks_d, 8), "sinks_bc"),
        ]
        if NWFOLD_:
            small.append((nw_bc[:, :], bcast_rows(normw_d, D_MODEL), "nw_bc"))
        load_tile(0)
        load_rot(0)
        for o_ap, i_ap, key in small:
            slow = key == "normw_col"
            S.op("sp", lambda e, o_ap=o_ap, i_ap=i_ap, slow=slow:
                 e.dma_start(out=o_ap, in_=i_ap, allow_slow_non_contiguous=slow), r=(), w=[key], dma=key)
        load_tile(1)

        win_v = win_d.rearrange("(c p) n -> p c n", p=128)
        for (c0, n) in (P_QK, P_V, P_SQ, P_SKV, P_RG, P_SG):
            dma("pool", w_in_bf[:, :, c0:c0 + n], win_v[:, :, c0:c0 + n], r=(), w=[f"win{c0}"], key=f"win{c0}")
        wout_v = wout_d.rearrange("(c p) n -> p c n", p=128)
        for hf in range(2):
            dma("pool", w_out_bf[:, :, hf * 512:(hf + 1) * 512], wout_v[:, :, hf * 512:(hf + 1) * 512],
                r=(), w=[f"wout{hf}"], key=f"wout{hf}")

        memset("dve", eps6[:, :], NORM_EPS, w=["eps"])
        memset("dve", eps5[:, :], GN_EPS, w=["eps"])
        memset("dve", ones32[:, :], 1.0, w=["ones32"])
        memset("dve", one1[:, :], 1.0, w=["one1"])
        memset("dve", Tst[:, :, :], 0.0, w=["Tst"])
        for i in range(2):
            memset("pool", Tb[i][:, :, :], 0.0, w=[f"Tb{i}"])
        for i in range(3):
            memset("pool", vext[i][:, :, :], 1.0, w=[f"vext{i}"])
        cp("dve", ident[:, :], ident_f[:, :], r=["ident_f"], w=["ident"])

        KWQ = ["kwq0", "kwq1"]
        def late_setup():
            tt("dve", qkw[:, :], qw_bc[:, :], kw_bc[:, :], ALU.mult, r=["qw_bc", "kw_bc"], w=["qkw"])
            S.op("dve", lambda e: e.tensor_reduce(mtmp[:, 0:1], qkw[:, :], AX.X, ALU.max, apply_absolute_value=True),
                 r=["qkw"], w=["mt0"])
            S.op("dve", lambda e: e.tensor_reduce(mtmp[:, 1:2], sinks_bc[:, :], AX.X, ALU.max), r=["sinks_bc"], w=["mt1"])
            stt(mtmp[:, 2:3], mtmp[:, 0:1], 8.0, mtmp[:, 1:2], ALU.mult, ALU.max, r=["mt0", "mt1"], w=["mt2"])
            ts("dve", negM[:, :], mtmp[:, 2:3], -1.0, None, ALU.mult, ALU.bypass, r=["mt2"], w=["negM"])
            for g_ in range(2):
                ts("dve", kwq_bc[:, g_ * 64:(g_ + 1) * 64], qkw[:, :], 0.125, None, ALU.mult, ALU.bypass,
                   r=["qkw"], w=[f"kwq{g_}"])
            KWQ = ["kwq0", "kwq1"]
            act(esink[:, :], sinks_bc[:, :], AF.Exp, r=["sinks_bc", "negM"], w=["esink"], bias=negM[:, 0:1], scale=1.0)

            tt("dve", rhsB[:, :, :], ohb[:, :].unsqueeze(1).to_broadcast([32, 8, 256]),
               rb32[:, :].unsqueeze(2).to_broadcast([32, 8, 256]), ALU.mult, r=["ohb", "rb32"], w=["rhsB"])
            for q in range(4):
                pb, pk = next_pf()
                mm(pb[:, :], ones32[:, :], rhsB[:, 2 * q:2 * q + 2, :].rearrange("p a b -> p (a b)"), True, True,
                   r=["ones32", "rhsB"], w=[pk])
                act(W_sb[:, 2 * q:2 * q + 2, :].rearrange("p a b -> p (a b)"), pb[:, :], AF.Copy, r=[pk], w=[f"W_sb{q}"])
            dst = bass.AP(scr_d.tensor, scr_d.offset, [[257, 128], [SCR_H, 8], [1, 256]])
            dma("sp", dst, W_sb[:, :, :], r=[f"W_sb{q}" for q in range(4)], w=["scr"], key="scrw")
            scr_w_key = "scr"
            Tfull = EBp
            src = bass.AP(scr_d.tensor, scr_d.offset + 128, [[256, 128], [SCR_H, 8], [1, 128]])
            dma("sp", Tfull[:, :, :], src, r=[scr_w_key], w=["EBp"], key="scrr")
            tt("dve", EBc[:, :, :], Tfull[:, :, :], maskc[:, :].unsqueeze(1).to_broadcast([128, 8, 128]), ALU.mult,
               r=["EBp", "maskc"], w=["EBc"])
            tt("pool", EBp[:, :, :], Tfull[:, :, :], maskp[:, :].unsqueeze(1).to_broadcast([128, 8, 128]), ALU.mult,
               r=["EBp", "maskp"], w=["EBp"])
            ts("dve", negc[:, :], maskc[:, :], 30000.0, -30000.0, ALU.mult, ALU.add, r=["maskc"], w=["negc"])
            ts("dve", negp[:, :], maskp[:, :], 30000.0, -30000.0, ALU.mult, ALU.add, r=["maskp"], w=["negp"])
            tt("dve", EBc[:, :, :], EBc[:, :, :], negc[:, :].unsqueeze(1).to_broadcast([128, 8, 128]), ALU.add,
               r=["EBc", "negc"], w=["EBc"])
            tt("pool", EBp[:, :, :], EBp[:, :, :], negp[:, :].unsqueeze(1).to_broadcast([128, 8, 128]), ALU.add,
               r=["EBp", "negp"], w=["EBp"])
            if "m" in BM:
                act(EMc[:, :, :].rearrange("p h i -> p (h i)"), EBc[:, :, :].rearrange("p h i -> p (h i)"), AF.Exp,
                    r=["EBc"], w=["EMc"])
                act(EMp[:, :, :].rearrange("p h i -> p (h i)"), EBp[:, :, :].rearrange("p h i -> p (h i)"), AF.Exp,
                    r=["EBp"], w=["EMp"])


        def front_gen(t):
            b2 = t % 2
            xk = f"xs{t % NXS}"
            r3 = t % 3
            ssk = f"ss{b2}"
            act(xn[b2][:, :], xs[t % NXS][:, :], AF.Square, r=[xk, f"rel{t - 2}"], w=[f"xn{b2}", ssk], accum=ss[b2][:, 0:1])
            rsqrt_chain(ss[b2][:, 0:1], ss[b2][:, 1:2], ss[b2][:, 2:3], 1.0 / D_MODEL, eps6, ssk)
            if NWFOLD:
                tt("pool", xw[:, :], xs[t % NXS][:, :], nw_bc[:, :], ALU.mult, r=[xk, "nw_bc"], w=["xw"])
                ts("pool", xn[b2][:, :], xw[:, :], ss[b2][:, 2:3], 1.0, ALU.mult, ALU.mult,
                   r=["xw", ssk], w=[f"xn{b2}"])
            else:
                ts("pool", xn[b2][:, :], xs[t % NXS][:, :], ss[b2][:, 2:3], 1.0, ALU.mult, ALU.mult,
                   r=[xk, ssk], w=[f"xn{b2}"])
            yield
            tb, tk = next_tp()
            for c in range(8):
                tr(tb[:, c * 128:(c + 1) * 128], xn[b2][:, c * 128:(c + 1) * 128], r=[f"xn{b2}"], w=[tk])
            for hh_ in range(XSPLIT):
                cw = 8 // XSPLIT
                src = tb[:, hh_ * cw * 128:(hh_ + 1) * cw * 128].rearrange("p (c t) -> p c t", c=cw)
                if NWFOLD:
                    cp("dve", xT[b2][:, hh_ * cw:(hh_ + 1) * cw, :], src, r=[tk], w=[f"xT{b2}_{hh_}"])
                else:
                    tt("dve", xT[b2][:, hh_ * cw:(hh_ + 1) * cw, :], src,
                       normw_col[:, hh_ * cw:(hh_ + 1) * cw].unsqueeze(2).to_broadcast([128, cw, 128]), ALU.mult,
                       r=[tk, "normw_col"], w=[f"xT{b2}_{hh_}"])
            yield

            def proj(piece, cls="pq"):
                c0, n = piece
                pb, pk = next_pf(cls)
                for c in range(8):
                    mm(pb[:, 0:n], xT[b2][:, c, :], w_in_bf[:, c, c0:c0 + n], c == 0, c == 7,
                       r=[f"xT{b2}_{c // (8 // XSPLIT)}", f"win{c0}"], w=[pk])
                return pb, pk

            pb, pk = proj(P_QK)
            csb = cs_t[r3][:, :].rearrange("p (h d) -> p h d", h=8).unsqueeze(2).to_broadcast([128, 8, 2, 32])
            snb = sn_t[r3][:, :].rearrange("p (h d) -> p h d", h=8).unsqueeze(2).to_broadcast([128, 8, 2, 32])
            pv4 = pb[:, :].rearrange("p (h f d) -> p h f d", h=8, f=2)
            t1v = t1[:, :].rearrange("p (h f d) -> p h f d", h=8, f=2)
            t2v = t2[:, :].rearrange("p (h f d) -> p h f d", h=8, f=2)
            qkv = qkr[b2][:, :].rearrange("p (h f d) -> p h f d", h=8, f=2)
            tt("dve", t1v, pv4, csb, ALU.mult, r=[pk, f"cs{r3}"], w=["t1"])
            tt("dve", t2v, pv4, snb, ALU.mult, r=[pk, f"sn{r3}"], w=["t2"])
            tt("pool", qkv[:, :, 0, :], t1v[:, :, 0, :], t2v[:, :, 1, :], ALU.subtract, r=["t1", "t2"], w=[f"qkr{b2}"])
            tt("pool", qkv[:, :, 1, :], t1v[:, :, 1, :], t2v[:, :, 0, :], ALU.add, r=["t1", "t2"], w=[f"qkr{b2}"])
            yield
            pb, pk = proj(P_V)
            act(vb[b2][:, :], pb[:, :], AF.Copy, r=[pk], w=[f"vb{b2}"])
            yield
            sk_ = f"ssq{b2}"
            pbq, pkq = proj(P_SQ, "pl")
            act(sqj[:, 0:512], pbq[:, :], AF.Square, r=[pkq], w=["sqj_a"])
            S.op("dve", lambda e, b2=b2: e.tensor_reduce(ssq[b2][:, 0:8], sqj[:, 0:512].rearrange("p (h d) -> p h d", h=8),
                                                        AX.X, ALU.add), r=["sqj_a"], w=[sk_], cost=0.65, lat=0.1)
            pbk, pkk = proj(P_SKV, "pl")
            act(sqj[:, 512:640], pbk[:, 0:128], AF.Square, r=[pkk], w=["sqj_b"])
            S.op("dve", lambda e, b2=b2: e.tensor_reduce(ssq[b2][:, 8:10], sqj[:, 512:640].rearrange("p (h d) -> p h d", h=2),
                                                        AX.X, ALU.add), r=["sqj_b"], w=[sk_], cost=0.2, lat=0.1)
            vx = vext[r3]
            VXL = int(os.environ.get("K_VXL", "0"))
            if not VXL:
                act(vx[:, :, 0:64], pbk[:, 128:256].rearrange("p (g d) -> p g d", g=2), AF.Copy,
                    r=[pkk], w=[f"vext{r3}"])
            rsqrt_chain(ssq[b2][:, 0:10], ssq[b2][:, 10:20], ssq[b2][:, 20:30], 1.0 / 64, eps6, sk_)
            if VXL:
                act(vx[:, :, 0:64], pbk[:, 128:256].rearrange("p (g d) -> p g d", g=2), AF.Copy,
                    r=[pkk], w=[f"vext{r3}"])
            rrq = ssq[b2][:, 20:28].rearrange("p (g h) -> p h g", g=2).unsqueeze(3).to_broadcast([128, 4, 2, 64])
            tt("dve", qn[b2][:, :].rearrange("p (h g d) -> p h g d", h=4, g=2),
               pbq[:, :].rearrange("p (g h d) -> p h g d", g=2, h=4), rrq, ALU.mult,
               r=[pkq, sk_], w=[f"qn{b2}"])
            for g in range(2):
                stt(kn[b2][:, g * 64:(g + 1) * 64], pbk[:, g * 64:(g + 1) * 64], ssq[b2][:, 28 + g:29 + g],
                    kwq_bc[:, g * 64:(g + 1) * 64], ALU.mult, ALU.mult, r=[pkk, sk_] + KWQ, w=[f"kn{b2}"])
            yield
            tb, tk = next_tp()
            for p in range(4):
                tr(tb[:, p * 128:(p + 1) * 128], qkr[b2][:, p * 128:(p + 1) * 128], r=[f"qkr{b2}"], w=[tk])
            evac(CPE[0], qkT[b2][:, :, :], tb[:, 0:512].rearrange("p (a t) -> p a t", a=4), r=[tk], w=[f"qkT{b2}"])
            yield
            GV = os.environ.get("K_GATE", "pool")

            def gate(dst, dkey, pb, pk):
                if GV == "silu":
                    act(dst, pb[:, :], AF.Silu, r=[pk], w=[dkey])
                    return
                act(dst, pb[:, :], AF.Exp, r=[pk], w=[dkey], scale=-1.0)
                act(dst, dst, AF.Ln, r=[dkey, "one1"], w=[dkey], bias=one1[:, 0:1], scale=1.0)
                act(dst, dst, AF.Exp, r=[dkey], w=[dkey], scale=-1.0)
                if GV == "dve":
                    tt("dve", dst, pb[:, :], dst, ALU.mult, r=[pk, dkey], w=[dkey])
                else:
                    act(gtmp[:, :], pb[:, :], AF.Copy, r=[pk], w=["gtmp"])
                    tt("pool", dst, dst, gtmp[:, :], ALU.mult, r=[dkey, "gtmp"], w=[dkey])

            pb, pk = proj(P_RG)
            gate(gate_r[b2][:, :], f"gate_r{b2}", pb, pk)
            tt("pool", gate_r[b2][:, :], gate_r[b2][:, :], retw_bc[:, :], ALU.mult,
               r=[f"gate_r{b2}", "retw_bc"], w=[f"gate_r{b2}"])
            yield
            pb, pk = proj(P_SG)
            gate(gate_s[b2][:, :], f"gate_s{b2}", pb, pk)
            yield
            tb, tk = next_tp()
            for hg in range(4):
                tr(tb[:, hg * 128:(hg + 1) * 128], qn[b2][:, hg * 128:(hg + 1) * 128], r=[f"qn{b2}"], w=[tk])
            tr(tb[:, 512:640], kn[b2][:, :], r=[f"kn{b2}"], w=[tk])
            evac(CPE[1], qnT[b2][:, :], tb[:, 0:512], r=[tk], w=[f"qnT{b2}"])
            evac(CPE[2], knT[r3][:, :], tb[:, 512:640], r=[tk], w=[f"knT{r3}"])
            yield

        def back_gen(t):
            b2 = t % 2
            xk = f"xs{t % NXS}"
            qT_ = qkT[b2]
            r3 = t % 3
            p3 = (t - 1) % 3
            dk = f"den{b2}"
            for which in (0, 1):
                if which == 1 and t == 0:
                    continue
                for g in range(2):
                    lo = g * 64
                    kt = knT[r3] if which == 0 else knT[p3]
                    kkey = f"knT{r3}" if which == 0 else f"knT{p3}"
                    EB = EBc if which == 0 else EBp
                    ekey = "EBc" if which == 0 else "EBp"
                    pcb, pck = next_pf("bs")
                    mm(pcb[:, :], kt[lo:lo + 64, :], qnT[b2][lo:lo + 64, :], True, True,
                       r=[kkey, f"qnT{b2}"], w=[pck])
                    i_ = 2 * g + which
                    if BM[i_] == "a":
                        tt("dve", Praw[i_][:, :], pcb[:, :], EB[:, g * 4:(g + 1) * 4, :].rearrange("p h i -> p (h i)"),
                           ALU.add, r=[pck, ekey], w=[f"Praw{i_}"])
                        act(Pbf[i_][:, :], Praw[i_][:, :], AF.Exp, r=[f"Praw{i_}", "negM"], w=[f"Pbf{i_}"],
                            bias=negM[:, 0:1], scale=1.0)
                    else:
                        EM = EMc if which == 0 else EMp
                        act(Praw[i_][:, :], pcb[:, :], AF.Exp, r=[pck, "negM"], w=[f"Praw{i_}"],
                            bias=negM[:, 0:1], scale=1.0)
                        tt("pool", Pbf[i_][:, :], Praw[i_][:, :],
                           EM[:, g * 4:(g + 1) * 4, :].rearrange("p h i -> p (h i)"), ALU.mult,
                           r=[f"Praw{i_}", "EMc" if which == 0 else "EMp"], w=[f"Pbf{i_}"])
            release("B3", t)
            yield
            psbs = [next_pf("bs"), next_pf("bs")]
            for h in range(4):
                lo = (h % 2) * 64
                psb, psk = psbs[h % 2]
                mm(psb[:, (h // 2) * 128:(h // 2 + 1) * 128], qT_[lo:lo + 64, 2 + h // 2, :], qT_[lo:lo + 64, h // 2, :],
                   True, True, r=[f"qkT{b2}"], w=[psk])
            sT4 = sT[b2][:, :].rearrange("p (a hh i) -> p a hh i", a=2, hh=2)
            for hh in range(2):
                psb, psk = psbs[hh]
                tt("dve", sT4[:, :, hh, :], psb[:, 0:256].rearrange("p (a i) -> p a i", a=2),
                   maskc[:, :].unsqueeze(1).to_broadcast([128, 2, 128]), ALU.mult, r=[psk, "maskc"], w=[f"sT{b2}"])
            yield
            pkvb, pkvk = next_pf("bl")
            for p in range(2):
                mm(pkvb[:, p * 256:(p + 1) * 256], qkr[b2][:, 256 + p * 128:256 + (p + 1) * 128],
                   vb[b2][:, p * 256:(p + 1) * 256], True, True, r=[f"qkr{b2}", f"vb{b2}"], w=[pkvk])
            prb, prk = next_pf("bl")
            tbk = f"Tb{b2}"
            for p in range(2):
                if t > 0:
                    mm(prb[:, p * 256:(p + 1) * 256], qT_[:, p, :], Tb[b2][:, p, :], True, False,
                       r=[f"qkT{b2}", tbk], w=[prk])
                for hh in range(2):
                    h = 2 * p + hh
                    mm(prb[:, h * 128:(h + 1) * 128], sT[b2][:, h * 128:(h + 1) * 128], vb[b2][:, h * 128:(h + 1) * 128],
                       t == 0, (t == 0) or hh == 1, r=[f"sT{b2}", f"vb{b2}"], w=[prk])
            STL = int(os.environ.get("K_STL", "1"))
            def _state_update():
                if t + 1 < NT:
                    for p in range(2):
                        for hh in range(2):
                            lo = hh * 64
                            stt(Tst[lo:lo + 64, p, hh * 128:(hh + 1) * 128], Tst[lo:lo + 64, p, hh * 128:(hh + 1) * 128],
                                gC[lo:lo + 64, p:p + 1], pkvb[lo:lo + 64, p * 256 + hh * 128:p * 256 + (hh + 1) * 128],
                                ALU.mult, ALU.add, r=["Tst", "gC", pkvk], w=["Tst"])
                    nb = (t + 1) % 2
                    for p in range(2):
                        ts("pool", Tb[nb][:, p, :], Tst[:, p, :], gC[:, p:p + 1], 1.0, ALU.mult, ALU.mult,
                           r=["Tst", "gC"], w=[f"Tb{nb}"])
            if not STL:
                _state_update()
            mvk = f"bmv{b2}"
            for h in range(4):
                S.op("dve", lambda e, h=h, prb=prb: e.bn_stats(bst[:, h, :], R(prb[:, h * 128:(h + 1) * 128])),
                     r=[prk], w=["bst"], cost=0.25, lat=0.1)
            for h in range(4):
                S.op("dve", lambda e, h=h, b2=b2: e.bn_aggr(bmv[b2][:, 2 * h:2 * h + 2], bst[:, h, :]),
                     r=["bst"], w=[mvk, f"bmvm{b2}"], cost=0.1, lat=0.1)
            mv3 = bmv[b2][:, 0:8].rearrange("p (h s) -> p h s", s=2)
            act(bmv[b2][:, 8:12], mv3[:, :, 1], AF.Ln, r=[mvk, "eps"], w=[mvk], bias=eps5[:, 0:1], scale=1.0)
            act(bmv[b2][:, 12:16], bmv[b2][:, 8:12], AF.Exp, r=[mvk], w=[mvk], scale=-0.5)
            if B2S:
                for h in range(4):
                    stt(ybuf[:, h * 128:(h + 1) * 128], prb[:, h * 128:(h + 1) * 128], bmv[b2][:, 2 * h:2 * h + 1],
                        gate_r[b2][:, h * 128:(h + 1) * 128], ALU.subtract, ALU.mult,
                        r=[prk, f"bmvm{b2}", f"gate_r{b2}"], w=[f"ybuf{h}"])
                for h in range(4):
                    ts(B2E, mixed[b2][:, h * 128:(h + 1) * 128], ybuf[:, h * 128:(h + 1) * 128],
                       bmv[b2][:, 12 + h:13 + h], 1.0, ALU.mult, ALU.mult, r=[f"ybuf{h}", mvk], w=[f"mixed{b2}"])
            else:
                for h in range(4):
                    ts("dve", ybuf[:, h * 128:(h + 1) * 128], prb[:, h * 128:(h + 1) * 128], bmv[b2][:, 2 * h:2 * h + 1],
                       bmv[b2][:, 12 + h:13 + h], ALU.subtract, ALU.mult, r=[prk, mvk], w=["ybuf"])
                tt("pool", mixed[b2][:, 0:512], ybuf[:, :], gate_r[b2][:, :], ALU.mult,
                   r=["ybuf", f"gate_r{b2}"], w=[f"mixed{b2}"])
            release("B2", t)
            if STL:
                _state_update()
            yield
            for g in range(2):
                i_c, i_p = 2 * g, 2 * g + 1
                pob, pok = next_pf("bl")
                po3 = pob[:, 0:260].rearrange("p (h e) -> p h e", h=4)
                for hg in range(4):
                    mm(po3[:, hg, :], Pbf[i_c][:, hg * 128:(hg + 1) * 128], vext[r3][:, g, :], True, t == 0,
                       r=[f"Pbf{i_c}", f"vext{r3}"], w=[pok])
                    if t > 0:
                        mm(po3[:, hg, :], Pbf[i_p][:, hg * 128:(hg + 1) * 128], vext[p3][:, g, :], False, True,
                           r=[f"Pbf{i_p}", f"vext{p3}"], w=[pok])
                tt("dve", den[b2][:, g * 4:(g + 1) * 4], po3[:, :, 64], esink[:, g * 4:(g + 1) * 4], ALU.add,
                   r=[pok, "esink"], w=[dk])
                S.op("dve", lambda e, b2=b2, g=g: e.reciprocal(den[b2][:, 8 + g * 4:8 + (g + 1) * 4],
                                                               den[b2][:, g * 4:(g + 1) * 4]), r=[dk], w=[dk], cost=0.18, lat=0.1)
                if B4S:
                    tt("dve", gs2[:, g * 256:(g + 1) * 256].rearrange("p (h d) -> p h d", h=4), po3[:, :, 0:64],
                       gate_s[b2][:, g * 256:(g + 1) * 256].rearrange("p (h d) -> p h d", h=4), ALU.mult,
                       r=[pok, f"gate_s{b2}"], w=[f"gs2_{g}"])
                    tt(B4E, mixed[b2][:, 512 + g * 256:512 + (g + 1) * 256].rearrange("p (h d) -> p h d", h=4),
                       gs2[:, g * 256:(g + 1) * 256].rearrange("p (h d) -> p h d", h=4),
                       den[b2][:, 8 + g * 4:8 + (g + 1) * 4].unsqueeze(2).to_broadcast([128, 4, 64]), ALU.mult,
                       r=[f"gs2_{g}", dk], w=[f"mixed{b2}"])
                else:
                    tt(os.environ.get("K_GS2E", "dve"), gs2[:, g * 256:(g + 1) * 256].rearrange("p (h d) -> p h d", h=4),
                       gate_s[b2][:, g * 256:(g + 1) * 256].rearrange("p (h d) -> p h d", h=4),
                       den[b2][:, 8 + g * 4:8 + (g + 1) * 4].unsqueeze(2).to_broadcast([128, 4, 64]), ALU.mult,
                       r=[f"gate_s{b2}", dk], w=["gs2"])
                    tt("dve", mixed[b2][:, 512 + g * 256:512 + (g + 1) * 256].rearrange("p (h d) -> p h d", h=4),
                       po3[:, :, 0:64], gs2[:, g * 256:(g + 1) * 256].rearrange("p (h d) -> p h d", h=4), ALU.mult,
                       r=[pok, "gs2"], w=[f"mixed{b2}"])
            release("B4", t)
            yield
            tb, tk = next_tp("tB")
            for c in range(8):
                tr(tb[:, c * 128:(c + 1) * 128], mixed[b2][:, c * 128:(c + 1) * 128], r=[f"mixed{b2}"], w=[tk])
            for hh_ in range(MSPLIT):
                cw = 8 // MSPLIT
                evac(CPE[3], mT[:, hh_ * cw:(hh_ + 1) * cw, :],
                     tb[:, hh_ * cw * 128:(hh_ + 1) * cw * 128].rearrange("p (c t) -> p c t", c=cw),
                     r=[tk], w=[f"mT{hh_}"])
            yield
            for hf in range(2):
                pb, pk = next_pf("bo")
                for c in range(8):
                    mm(pb[:, :], mT[:, c, :], w_out_bf[:, c, hf * 512:(hf + 1) * 512], c == 0, c == 7,
                       r=[f"mT{c // (8 // MSPLIT)}", f"wout{hf}"], w=[pk])
                tt("dve", ot[b2][:, hf * 512:(hf + 1) * 512], pb[:, :], xs[t % NXS][:, hf * 512:(hf + 1) * 512], ALU.add,
                   r=[pk, xk], w=[f"ot{b2}_{hf}"])
                if OSPLIT:
                    dma("sp", out_d[t * 128:(t + 1) * 128, hf * 512:(hf + 1) * 512], ot[b2][:, hf * 512:(hf + 1) * 512],
                        r=[f"ot{b2}_{hf}"], w=[f"out{t}_{hf}"], key=f"ot{b2}_{hf}")
            if not OSPLIT:
                dma("sp", out_d[t * 128:(t + 1) * 128, :], ot[b2][:, :], r=[f"ot{b2}_0", f"ot{b2}_1"],
                    w=[f"out{t}_0", f"out{t}_1"], key=f"ot{b2}")
            yield

        F = {t: front_gen(t) for t in range(NT)}
        B = {t: back_gen(t) for t in range(NT)}

        def step(g):
            next(g, None)

        if NT > 2:
            load_tile(2)
        if NT > 1:
            load_rot(1)
        late_setup()
        step(F[0])
        if NT > 1:
            step(F[1])
        for _ in range(8):
            step(F[0])
        ORDER = os.environ.get("K_ORDER", "FBNBFBFBFFFBFFB")
        for t in range(NT):
            if t + 3 < NT:
                load_tile(t + 3)
            if t + 2 < NT:
                load_rot(t + 2)
            for o in ORDER:
                if o == "B":
                    step(B[t])
                elif o == "N":
                    if t + 2 < NT:
                        step(F[t + 2])
                elif t + 1 < NT:
                    step(F[t + 1])


        fin_r = [f"out{t}_{hf}" for t in range(NT) for hf in range(2)]
        S.op("sp", None, r=fin_r, w=())
        S.emit(nc, es)
    return nc


_CONSTS = None


def kernel(x, norm_w, w_in, ret_norm_w, q_norm_w, k_norm_w, sinks, rel_bias, w_out, _dbg_tile=None):
    global _CONSTS
    if _CONSTS is None:
        _CONSTS = _consts()
    f = lambda a: np.ascontiguousarray(np.asarray(a, dtype=np.float32))
    x = f(x)
    shared = {
        "norm_w": f(norm_w), "w_in": f(w_in), "ret_norm_w": f(ret_norm_w), "q_norm_w": f(q_norm_w),
        "k_norm_w": f(k_norm_w), "sinks": f(sinks), "rel_bias": f(rel_bias), "w_out": f(w_out),
    }
    shared.update(_CONSTS)
    nc = build_nc(_dbg_tile)
    in_maps = [dict(shared, x=np.ascontiguousarray(x[i])) for i in range(N_CORES)]
    res = run_bass_kernel_spmd(nc, in_maps, core_ids=list(range(N_CORES)))
    out = np.stack([np.asarray(r["out"], dtype=np.float32).reshape(SEQ, D_MODEL) for r in res.results], axis=0)
    if _dbg_tile is not None:
        return out, res.results
    return out
```

```python
import math
import os
from contextlib import ExitStack

import numpy as np

import concourse.bass as bass
import concourse.mybir as mybir
from concourse.bass_utils import run_bass_kernel_spmd

F32 = mybir.dt.float32
BF16 = mybir.dt.bfloat16
AF = mybir.ActivationFunctionType
ALU = mybir.AluOpType
AX = mybir.AxisListType

D_MODEL = 1024
SEQ = 2048
NT = SEQ // 128
D_IN = 2816
N_CORES = 8
NORM_EPS = 1e-6
GN_EPS = 1e-5

P_QK = (0, 512)
P_V = (512, 512)
P_RG = (1024, 512)
P_SQ = (1536, 512)
P_SKV = (2048, 256)
P_SG = (2304, 512)


class PG:
    def __init__(self, kind, key, default):
        self.kind, self.key, self.default, self.t = kind, key, default, None

    def __getitem__(self, idx):
        return LazyAP(self, lambda t, idx=idx: t[idx])


class LazyAP:
    def __init__(self, g, fn):
        self.g, self.fn = g, fn

    def ap(self):
        return self.fn(self.g.t if self.g.t is not None else self.g.default)

    @property
    def shape(self):
        return self.fn(self.g.default).shape

    @property
    def dtype(self):
        return self.fn(self.g.default).dtype

    def __getitem__(self, idx):
        return LazyAP(self.g, lambda t, f=self.fn, idx=idx: f(t)[idx])

    def rearrange(self, pat, **kw):
        return LazyAP(self.g, lambda t, f=self.fn: f(t).rearrange(pat, **kw))


def R(a):
    return a.ap() if isinstance(a, LazyAP) else a


class Sched:
    SEM_LAT = 0.12
    TBL_SWITCH = 1.3
    WINDOW = int(os.environ.get("K_WINDOW", "360"))

    def __init__(self):
        self.ops = []
        self.lw = {}
        self.rd = {}
        self.groups = {}
        self.pgs = {}
        self.banks = {}

    def op(self, eng, fn, r=(), w=(), dma=None, cost=0.1, lat=0.0):
        idx = len(self.ops)
        deps = {}
        for k in r:
            if k in self.lw:
                deps[self.lw[k]] = "raw"
            if k[:2] in ("pf", "tp"):
                for x in self.rd.get(k, ()):
                    if self.ops[x]["eng"] != eng:
                        deps.setdefault(x, "rr")
        for k in w:
            if k in self.lw:
                deps.setdefault(self.lw[k], "waw")
            for x in self.rd.get(k, ()):
                deps.setdefault(x, "war")
        for k in r:
            self.rd.setdefault(k, []).append(idx)
        for k in w:
            self.lw[k] = idx
            self.rd[k] = []
        self.ops.append(dict(eng=eng, fn=fn, alldeps=deps, dma=dma, cost=cost, lat=lat, signal=dma is not None))
        for k in set(list(r) + list(w)):
            if "#" in k:
                self.groups.setdefault(k, []).append(idx)
        return idx

    def _needs_sem(self, P, C, kind):
        if P["dma"] is not None:
            return True
        if P["eng"] != C["eng"]:
            return True
        if C["dma"] is not None:
            return True
        if C["eng"] == "pe":
            return False
        return kind == "raw"

    def evaluate(self, order):
        ops = self.ops
        ptr = {e: 0 for e in order}
        eng_free = {e: 0.0 for e in order}
        fin = {}
        endi = {}
        dma_free = 0.0
        cur_tbl = "exp"
        left = sum(len(v) for v in order.values())
        while left:
            progressed = False
            for e, lst in order.items():
                while ptr[e] < len(lst):
                    i = lst[ptr[e]]
                    o = ops[i]
                    if any(d not in fin for d in o["alldeps"]):
                        break
                    st = eng_free[e]
                    for d, kind in o["alldeps"].items():
                        P = ops[d]
                        if P["eng"] == e and not self._needs_sem(P, o, kind):
                            st = max(st, endi[d])
                        else:
                            st = max(st, fin[d] + self.SEM_LAT)
                    c = o["true_cost"]
                    if o["dma"] is not None:
                        issue = 1.0 if e == "pool" else 0.06
                        eng_free[e] = st + issue
                        t0 = max(st + issue, dma_free)
                        dma_free = t0 + c
                        fin[i] = dma_free + 2.0
                        endi[i] = st + issue
                    else:
                        if e == "act" and o.get("tbl") and o["tbl"] != cur_tbl:
                            c += self.TBL_SWITCH
                            cur_tbl = o["tbl"]
                        eng_free[e] = st + c
                        endi[i] = st + c
                        fin[i] = st + c + o["lat"]
                    ptr[e] += 1
                    left -= 1
                    progressed = True
            assert progressed, "evaluate: deadlock"
        return max(fin.values())

    def schedule(self):
        ops = self.ops
        n = len(ops)
        import random as _random
        _rng = _random.Random(int(os.environ.get("K_SEED", "0")))
        _nz = float(os.environ.get("K_NOISE", "0"))
        noise = [(_rng.uniform(0.0, _nz) if _nz > 0 else 0.0) for _ in range(n)]
        pes = float(os.environ.get("K_PESCALE", "1.0"))
        dvs = float(os.environ.get("K_DVESCALE", "0.95"))
        for o in ops:
            o.setdefault("true_cost", o["cost"])
            if o["eng"] == "pe":
                o["cost"] = o["true_cost"] * pes
            elif o["eng"] == "dve":
                o["cost"] = o["true_cost"] * dvs
        self.groups_of = {}
        for k, lst in self.groups.items():
            for i in lst:
                self.groups_of.setdefault(i, []).append(k)
        succ = [[] for _ in range(n)]
        indeg = [0] * n
        for i, o in enumerate(ops):
            for d in o["alldeps"]:
                succ[d].append(i)
                indeg[i] += 1
        finish = [0.0] * n
        end_issue = [0.0] * n
        done = [False] * n

        def dep_ready(o, e, d, kind):
            P = ops[d]
            if P["eng"] == e and not self._needs_sem(P, o, kind):
                return end_issue[d]
            return finish[d] + self.SEM_LAT
        eng_free = {}
        ready = {}
        for i, o in enumerate(ops):
            eng_free.setdefault(o["eng"], 0.0)
            ready.setdefault(o["eng"], [])
            if indeg[i] == 0:
                ready[o["eng"]].append(i)
        order = {e: [] for e in ready}
        nsched = 0
        low = 0
        dma_free = 0.0
        cur_tbl = ["exp"]
        first_of = {lst[0]: k for k, lst in self.groups.items()}
        remaining = {k: len(lst) for k, lst in self.groups.items()}
        bind_q = {}
        for k in sorted(self.groups, key=lambda k: self.groups[k][0]):
            bind_q.setdefault(self.pgs[k].kind, []).append(k)
        bind_ptr = {kind: 0 for kind in bind_q}
        bank_state = {kind: [dict(group=None, free=0.0) for _ in tl] for kind, tl in self.banks.items()}

        def bank_for(i):
            k = first_of.get(i)
            if k is None:
                return None
            kind = self.pgs[k].kind
            if bind_q[kind][bind_ptr[kind]] != k:
                return (kind, -1, 0.0)
            best_b = -1
            for b, bs in enumerate(bank_state[kind]):
                if bs["group"] is None or remaining[bs["group"]] == 0:
                    if best_b < 0 or bs["free"] < bank_state[kind][best_b]["free"]:
                        best_b = b
            return (kind, best_b, bank_state[kind][best_b]["free"] if best_b >= 0 else 0.0)
        while nsched < n:
            while low < n and done[low]:
                low += 1
            best = None
            for e, lst in ready.items():
                for i in lst:
                    if i > low + self.WINDOW:
                        continue
                    o = ops[i]
                    st = eng_free[e]
                    bk = bank_for(i)
                    if bk is not None:
                        if bk[1] < 0:
                            continue
                        st = max(st, bk[2] + self.SEM_LAT)
                    for d, kind in o["alldeps"].items():
                        f = dep_ready(o, e, d, kind)
                        if f > st:
                            st = f
                    pen = self.TBL_SWITCH if (o.get("tbl") and o["tbl"] != cur_tbl[0]) else 0.0
                    key = (st + pen + noise[i], i)
                    if best is None or key < best[0]:
                        best = (key, e, i)
            if best is None:
                cand = sorted((i, e) for e, lst in ready.items() for i in lst)
                for i, e in cand:
                    bk = bank_for(i)
                    if bk is not None and bk[1] < 0:
                        continue
                    st = max([eng_free[e]] + [dep_ready(ops[i], e, d, kd) for d, kd in ops[i]["alldeps"].items()]
                             + ([bk[2] + self.SEM_LAT] if bk is not None else []))
                    best = ((st, i), e, i)
                    break
                assert best is not None, "scheduler deadlock (PSUM banks)"
            (st, i), e, _ = best
            o = ops[i]
            tsw = 0.0
            if o.get("tbl") and o["tbl"] != cur_tbl[0]:
                st -= self.TBL_SWITCH if best[0][0] - self.TBL_SWITCH >= eng_free[e] - 1e-9 else 0.0
                st = max(st, eng_free[e])
                tsw = self.TBL_SWITCH
                cur_tbl[0] = o["tbl"]
            ready[e].remove(i)
            bk = bank_for(i)
            if bk is not None:
                kind, b, _f = bk
                k = first_of[i]
                prevg = bank_state[kind][b]["group"]
                if prevg is not None:
                    for j in self.groups[prevg]:
                        o["alldeps"].setdefault(j, "war")
                bank_state[kind][b]["group"] = k
                self.pgs[k].t = self.banks[kind][b]
                bind_ptr[kind] += 1
            bind = ("eng", order[e][-1] if order[e] else -1)
            if not (order[e] and abs(eng_free[e] - st) < 1e-9):
                for d, kind in o["alldeps"].items():
                    f = dep_ready(o, e, d, kind)
                    if abs(f - st) < 1e-9:
                        bind = ("dep", d)
            o["bind"], o["st"] = bind, st
            if o["dma"] is not None:
                issue = 1.0 if e == "pool" else 0.06
                eng_free[e] = st + issue
                t0 = max(st + issue, dma_free)
                dma_free = t0 + o["cost"]
                finish[i] = dma_free + 2.0
                end_issue[i] = st + issue
            else:
                eng_free[e] = st + o["cost"] + tsw
                end_issue[i] = st + o["cost"] + tsw
                finish[i] = st + o["cost"] + tsw + o["lat"]
            done[i] = True
            order[e].append(i)
            nsched += 1
            for k in self.groups_of.get(i, ()):
                remaining[k] -= 1
                if remaining[k] == 0:
                    kind = self.pgs[k].kind
                    for bs in bank_state[kind]:
                        if bs["group"] == k:
                            bs["free"] = max(finish[j] for j in self.groups[k])
            for j in succ[i]:
                indeg[j] -= 1
                if indeg[j] == 0:
                    ready[ops[j]["eng"]].append(j)
        self.makespan = max(finish)
        self.finish = finish
        self.true_makespan = self.evaluate(order)
        return order

    def emit(self, nc, es):
        ops = self.ops
        order = self.schedule()
        for i, o in enumerate(ops):
            o["deps"] = []
            for d, kind in o["alldeps"].items():
                if self._needs_sem(ops[d], o, kind):
                    o["deps"].append(d)
                    ops[d]["signal"] = True
        eng_count = {}
        dma_count = {}
        for e, lst in order.items():
            for i in lst:
                o = ops[i]
                if not o["signal"]:
                    continue
                if o["dma"] is not None:
                    k = "D_" + o["dma"]
                    dma_count[k] = dma_count.get(k, 0) + 16
                    o["sem"], o["val"] = k, dma_count[k]
                else:
                    k = "E_" + o["eng"]
                    eng_count[k] = eng_count.get(k, 0) + 1
                    o["sem"], o["val"] = k, eng_count[k]
        semh = {}
        for k in list(eng_count) + list(dma_count):
            semh[k] = es.enter_context(nc.semaphore(k))

        def mk(engname):
            def body(engine):
                waited = {}
                for i in order.get(engname, []):
                    o = ops[i]
                    for d in sorted(o["deps"], key=lambda d: (ops[d]["sem"], ops[d]["val"])):
                        P = ops[d]
                        s, v = P["sem"], P["val"]
                        if waited.get(s, 0) >= v:
                            continue
                        engine.wait_ge(semh[s], v)
                        waited[s] = v
                    if o["fn"] is not None:
                        ins = o["fn"](engine)
                        if o["signal"]:
                            ins.then_inc(semh[o["sem"]], 16 if o["dma"] is not None else 1)
            return body

        with nc.Block() as block:
            block.tensor(mk("pe"))
            block.scalar(mk("act"))
            block.vector(mk("dve"))
            block.gpsimd(mk("pool"))
            block.sync(mk("sp"))


def _t5_bucket_np(n):
    n = np.asarray(n, dtype=np.int32)
    max_exact = 16
    nf = np.maximum(n, 1).astype(np.float32)
    large = max_exact + (np.log(nf / np.float32(max_exact)) / np.float32(math.log(128 / max_exact))
                         * np.float32(32 - max_exact)).astype(np.int32)
    large = np.minimum(large, 31)
    return np.where(n < max_exact, n, large)


def _consts():
    c = {}
    c["ident"] = np.eye(128, dtype=np.float32)
    jj = np.arange(128)[:, None]
    ii = np.arange(128)[None, :]
    c["maskc"] = (ii >= jj).astype(np.float32)
    c["maskp"] = (ii < jj).astype(np.float32)
    u = np.arange(256) % 128
    bk = _t5_bucket_np(u)
    ohb = np.zeros((32, 256), np.float32)
    ohb[bk, np.arange(256)] = 1.0
    c["ohb"] = ohb
    half = 32
    inv_freq = 10000.0 ** (-np.arange(half, dtype=np.float64) / half)
    pos = np.arange(SEQ, dtype=np.float64)
    ang = pos[:, None] * inv_freq[None, :]
    cos = np.cos(ang)
    sin = np.sin(ang)
    h = np.arange(4, dtype=np.float64)
    gamma = 1.0 - np.exp2(-5.0 - h)
    r = (np.arange(SEQ) % 128 + 1).astype(np.float64)
    sq = gamma[None, :] ** r[:, None]
    sk = gamma[None, :] ** (-r[:, None]) * (64.0 ** -0.5)
    s = np.concatenate([sq, sk], axis=1)
    c["cs"] = (cos[:, None, :] * s[:, :, None]).reshape(SEQ, 256).astype(np.float32)
    c["sn"] = (sin[:, None, :] * s[:, :, None]).reshape(SEQ, 256).astype(np.float32)
    gC = np.zeros((128, 2), np.float64)
    for pair in range(2):
        gC[:64, pair] = gamma[2 * pair] ** 128
        gC[64:, pair] = gamma[2 * pair + 1] ** 128
    c["gC"] = gC.astype(np.float32)
    return c


def build_nc(dbg_tile=None, NT=NT):
    SEQ = NT * 128
    nc = bass.Bass("TRN2", target_bir_lowering=False)
    S = Sched()

    def din(name, shape):
        return nc.dram_tensor(name, list(shape), F32, kind="ExternalInput").ap()

    x_d = din("x", [SEQ, D_MODEL])
    normw_d = din("norm_w", [D_MODEL])
    win_d = din("w_in", [D_MODEL, D_IN])
    retw_d = din("ret_norm_w", [512])
    qw_d = din("q_norm_w", [64])
    kw_d = din("k_norm_w", [64])
    sinks_d = din("sinks", [8])
    relb_d = din("rel_bias", [32, 8])
    wout_d = din("w_out", [D_MODEL, D_MODEL])
    ident_d = din("ident", [128, 128])
    maskc_d = din("maskc", [128, 128])
    maskp_d = din("maskp", [128, 128])
    ohb_d = din("ohb", [32, 256])
    cs_d = din("cs", [2048, 256])
    sn_d = din("sn", [2048, 256])
    gC_d = din("gC", [128, 2])
    out_d = nc.dram_tensor("out", [SEQ, D_MODEL], F32, kind="ExternalOutput").ap()
    SCR_H = 128 * 257
    scr_d = nc.dram_tensor("scr", [8 * SCR_H], F32, kind="Internal").ap()
    dbg_outs = {}

    with ExitStack() as es:
        def sb(name, shape, dt=F32):
            return es.enter_context(nc.sbuf_tensor("s_" + name, list(shape), dt))

        def psum(name, shape, dt=F32):
            return es.enter_context(nc.psum_tensor("p_" + name, list(shape), dt))

        def bcast_rows(d_ap, n):
            return bass.AP(d_ap.tensor, d_ap.offset, [[0, 128], [1, n]])

        w_in_bf = sb("w_in_bf", [128, 8, D_IN], BF16)
        w_out_bf = sb("w_out_bf", [128, 8, D_MODEL], BF16)
        ident_f = sb("ident_f", [128, 128])
        ident = sb("ident_b", [128, 128], BF16)
        maskc = sb("maskc", [128, 128])
        maskp = sb("maskp", [128, 128])
        negc = sb("negc", [128, 128])
        negp = sb("negp", [128, 128])
        ohb = sb("ohb", [32, 256])
        rb32 = sb("rb32", [32, 8])
        ones32 = sb("ones32", [32, 128])
        gC = sb("gC", [128, 2])
        normw_col = sb("normw_col", [128, 8])
        retw_bc = sb("retw_bc", [128, 512])
        qw_bc = sb("qw_bc", [128, 64])
        kw_bc = sb("kw_bc", [128, 64])
        kwq_bc = sb("kwq_bc", [128, 128])
        qkw = sb("qkw", [128, 64])
        sinks_bc = sb("sinks_bc", [128, 8])
        esink = sb("esink", [128, 8])
        mtmp = sb("mtmp", [128, 4])
        negM = sb("negM", [128, 1])
        eps6 = sb("eps6", [128, 1])
        eps5 = sb("eps5", [128, 1])
        one1 = sb("one1", [128, 1])
        gtmp = sb("gtmp", [128, 512])
        NWFOLD_ = int(os.environ.get("K_NWFOLD", "0"))
        if NWFOLD_:
            nw_bc = sb("nw_bc", [128, D_MODEL])
            xw = sb("xw", [128, D_MODEL])
        rhsB = sb("rhsB", [32, 8, 256])
        W_sb = sb("W_sb", [128, 8, 256])
        EBc = sb("EBc", [128, 8, 128])
        EBp = sb("EBp", [128, 8, 128])
        OSPLIT = int(os.environ.get("K_OSPLIT", "1"))
        B4S = int(os.environ.get("K_B4S", "0"))
        B4E = os.environ.get("K_B4E", "pool")
        B2S = int(os.environ.get("K_B2S", "1"))
        B2E = os.environ.get("K_B2E", "pool")
        BM = os.environ.get("K_BM", "mmam")
        if "m" in BM:
            EMc = sb("EMc", [128, 8, 128])
            EMp = sb("EMp", [128, 8, 128])

        NXS = 4
        xs = [sb(f"xs{i}", [128, D_MODEL]) for i in range(NXS)]
        cs_t = [sb(f"cs{i}", [128, 256]) for i in range(3)]
        sn_t = [sb(f"sn{i}", [128, 256]) for i in range(3)]
        ss = [sb(f"ss{i}", [128, 4]) for i in range(2)]
        xn = [sb(f"xn{i}", [128, D_MODEL], BF16) for i in range(2)]
        xT = [sb(f"xT{i}", [128, 8, 128], BF16) for i in range(2)]
        t1 = sb("t1", [128, 512])
        t2 = sb("t2", [128, 512])
        qkr = [sb(f"qkr{i}", [128, 512], BF16) for i in range(2)]
        qkT = [sb(f"qkT{i}", [128, 4, 128], BF16) for i in range(2)]
        vb = [sb(f"vb{i}", [128, 512], BF16) for i in range(2)]
        gate_r = [sb(f"gate_r{i}", [128, 512]) for i in range(2)]
        sqj = sb("sqj", [128, 640])
        ssq = [sb(f"ssq{i}", [128, 32]) for i in range(2)]
        qn = [sb(f"qn{i}", [128, 512], BF16) for i in range(2)]
        qnT = [sb(f"qnT{i}", [128, 512], BF16) for i in range(2)]
        kn = [sb(f"kn{i}", [128, 128], BF16) for i in range(2)]
        knT = [sb(f"knT{i}", [128, 128], BF16) for i in range(3)]
        vext = [sb(f"vext{i}", [128, 2, 65], BF16) for i in range(3)]
        gate_s = [sb(f"gate_s{i}", [128, 512]) for i in range(2)]
        sT = [sb(f"sT{i}", [128, 512], BF16) for i in range(2)]
        Tst = sb("Tst", [128, 2, 256])
        Tb = [sb(f"Tb{i}", [128, 2, 256], BF16) for i in range(2)]
        bst = sb("bst", [128, 4, 6])
        bmv = [sb(f"bmv{i}", [128, 16]) for i in range(2)]
        ybuf = sb("ybuf", [128, 512])
        mixed = [sb(f"mixed{i}", [128, D_MODEL], BF16) for i in range(2)]
        Praw = [sb(f"Praw{i}", [128, 512]) for i in range(4)]
        Pbf = [sb(f"Pbf{i}", [128, 512], BF16) for i in range(4)]
        den = [sb(f"den{i}", [128, 16]) for i in range(2)]
        gs2 = sb("gs2", [128, 512])
        mT = sb("mT", [128, 8, 128], BF16)
        ot = [sb(f"ot{i}", [128, D_MODEL]) for i in range(2)]

        tp = [psum(f"tp{i}", [128, 1024], BF16) for i in range(2)]
        pf = [psum(f"pf{i}", [128, 512], F32) for i in range(6)]
        cnt = {"tp": 0, "pf": 0}

        S.banks = {"tp": tp, "pf": pf}
        gcount = [0]

        def next_tp(cls=None):
            gcount[0] += 1
            g = PG("tp", f"tp#{gcount[0]}", tp[0])
            S.pgs[g.key] = g
            return g, g.key

        def next_pf(cls=None):
            gcount[0] += 1
            g = PG("pf", f"pf#{gcount[0]}", pf[0])
            S.pgs[g.key] = g
            return g, g.key

        def fsz(ap):
            n = 1
            for d in ap.shape[1:]:
                n *= d
            return n

        def is_ps(ap):
            return isinstance(ap, LazyAP) or "PSum" in type(ap.tensor).__name__

        def dma(eng, out, in_, r, w, key):
            nbytes = fsz(out) * out.shape[0] * 4
            S.op(eng, lambda e, out=out, in_=in_: e.dma_start(out=out, in_=in_), r=r, w=w, dma=key,
                 cost=nbytes / 290e3)

        def mm(out, lhsT, rhs, start, stop, r, w):
            n = fsz(rhs)
            c = 0.03 if n <= 65 else 0.06 if n <= 128 else 0.11 if n <= 256 else 0.22
            if rhs.dtype == F32:
                c *= 4
            S.op("pe", lambda e, out=out, lhsT=lhsT, rhs=rhs, start=start, stop=stop:
                 e.matmul(R(out), lhsT, rhs, start=start, stop=stop), r=r, w=w, cost=c, lat=0.15)

        def tr(out, in_, r, w):
            S.op("pe", lambda e, out=out, in_=in_: e.transpose(R(out), in_, ident[:, :]), r=list(r) + ["ident"], w=w,
                 cost=0.06, lat=0.15)

        def act(out, in_, func, r, w, bias=None, scale=None, accum=None):
            kw = {}
            if bias is not None:
                kw["bias"] = bias
            if scale is not None:
                kw["scale"] = scale
            if accum is not None:
                kw["accum_out"] = accum
            c = 0.2 + fsz(in_) / 1300.0 + (0.1 if accum is not None else 0.0)
            i_ = S.op("act", lambda e, out=out, in_=in_, func=func, kw=kw: e.activation(R(out), R(in_), func, **kw), r=r, w=w,
                      cost=c, lat=0.1)
            S.ops[i_]["tbl"] = "silu" if func == AF.Silu else ("exp" if func in (AF.Exp, AF.Ln) else None)

        def ew_cost(eng, out, ins):
            n = fsz(out)
            if eng == "pool":
                return 0.12 + n / (440.0 if len(ins) > 1 else 950.0)
            nsb = sum(1 for a in ins if not is_ps(a) and a.dtype == F32)
            return 0.07 + n / (425.0 if (len(ins) > 1 and nsb > 1) else 850.0)

        def tt(eng, out, in0, in1, op, r, w):
            S.op(eng, lambda e, out=out, in0=in0, in1=in1, op=op: e.tensor_tensor(R(out), R(in0), R(in1), op), r=r, w=w,
                 cost=ew_cost(eng, out, [in0, in1]), lat=0.1)

        def ts(eng, out, in0, s1, s2, op0, op1, r, w):
            S.op(eng, lambda e, out=out, in0=in0, s1=s1, s2=s2, op0=op0, op1=op1:
                 e.tensor_scalar(R(out), R(in0), s1, s2, op0, op1), r=r, w=w, cost=ew_cost(eng, out, [in0]), lat=0.1)

        def stt(out, in0, sc, in1, op0, op1, r, w):
            S.op("dve", lambda e, out=out, in0=in0, sc=sc, in1=in1, op0=op0, op1=op1:
                 e.scalar_tensor_tensor(R(out), R(in0), sc, R(in1), op0, op1), r=r, w=w,
                 cost=ew_cost("dve", out, [in0, in1]), lat=0.1)

        def cp(eng, out, in_, r, w):
            c = ew_cost(eng, out, [in_])
            if eng == "dve" and out.dtype == BF16 and in_.dtype == BF16:
                c = 0.1 + fsz(out) / 1750.0
            S.op(eng, lambda e, out=out, in_=in_: e.tensor_copy(R(out), R(in_)), r=r, w=w, cost=c, lat=0.1)

        CPE = os.environ.get("K_CPE", "aaad")
        XSPLIT = int(os.environ.get("K_XSPLIT", "2"))
        MSPLIT = int(os.environ.get("K_MSPLIT", "8"))
        NWFOLD = int(os.environ.get("K_NWFOLD", "0"))

        def evac(which, out, in_, r, w):
            if which == "a":
                act(out, in_, AF.Copy, r=r, w=w)
            else:
                cp("dve", out, in_, r=r, w=w)

        def memset(eng, ap, val, w):
            S.op(eng, lambda e, ap=ap, val=val: e.memset(ap, val), r=(), w=w, cost=0.1 + fsz(ap) / 2000.0)

        def rsqrt_chain(src, ln_out, rs_out, scale, eps_tile, key):
            act(ln_out, src, AF.Ln, r=[key, "eps"], w=[key], bias=eps_tile[:, 0:1], scale=scale)
            act(rs_out, ln_out, AF.Exp, r=[key], w=[key], scale=-0.5)

        def load_tile(t):
            dma("sp", xs[t % NXS][:, :], x_d[t * 128:(t + 1) * 128, :], r=(), w=[f"xs{t % NXS}"], key=f"xs{t % NXS}")

        def load_rot(t):
            dma("sp", cs_t[t % 3][:, :], cs_d[t * 128:(t + 1) * 128, :], r=(), w=[f"cs{t % 3}"], key=f"cs{t % 3}")
            dma("sp", sn_t[t % 3][:, :], sn_d[t * 128:(t + 1) * 128, :], r=(), w=[f"sn{t % 3}"], key=f"sn{t % 3}")

        small = [
            (ident_f[:, :], ident_d, "ident_f"), (maskc[:, :], maskc_d, "maskc"), (maskp[:, :], maskp_d, "maskp"),
            (ohb[:, :], ohb_d, "ohb"), (rb32[:, :], relb_d, "rb32"), (gC[:, :], gC_d, "gC"),
            (normw_col[:, :], normw_d.rearrange("(c p) -> p c", p=128), "normw_col"),
            (retw_bc[:, :], bcast_rows(retw_d, 512), "retw_bc"),
            (qw_bc[:, :], bcast_rows(qw_d, 64), "qw_bc"), (kw_bc[:, :], bcast_rows(kw_d, 64), "kw_bc"),
            (sinks_bc[:, :], bcast_rows(sinks_d, 8), "sinks_bc"),
        ]
        if NWFOLD_:
            small.append((nw_bc[:, :], bcast_rows(normw_d, D_MODEL), "nw_bc"))
        load_tile(0)
        load_rot(0)
        for o_ap, i_ap, key in small:
            slow = key == "normw_col"
            S.op("sp", lambda e, o_ap=o_ap, i_ap=i_ap, slow=slow:
                 e.dma_start(out=o_ap, in_=i_ap, allow_slow_non_contiguous=slow), r=(), w=[key], dma=key)
        load_tile(1)

        win_v = win_d.rearrange("(c p) n -> p c n", p=128)
        for (c0, n) in (P_QK, P_V, P_SQ, P_SKV, P_RG, P_SG):
            dma("pool", w_in_bf[:, :, c0:c0 + n], win_v[:, :, c0:c0 + n], r=(), w=[f"win{c0}"], key=f"win{c0}")
        wout_v = wout_d.rearrange("(c p) n -> p c n", p=128)
        for hf in range(2):
            dma("pool", w_out_bf[:, :, hf * 512:(hf + 1) * 512], wout_v[:, :, hf * 512:(hf + 1) * 512],
                r=(), w=[f"wout{hf}"], key=f"wout{hf}")

        memset("dve", eps6[:, :], NORM_EPS, w=["eps"])
        memset("dve", eps5[:, :], GN_EPS, w=["eps"])
        memset("dve", ones32[:, :], 1.0, w=["ones32"])
        memset("dve", one1[:, :], 1.0, w=["one1"])
        memset("dve", Tst[:, :, :], 0.0, w=["Tst"])
        for i in range(2):
            memset("pool", Tb[i][:, :, :], 0.0, w=[f"Tb{i}"])
        for i in range(3):
            memset("pool", vext[i][:, :, :], 1.0, w=[f"vext{i}"])
        cp("dve", ident[:, :], ident_f[:, :], r=["ident_f"], w=["ident"])

        KWQ = ["kwq0", "kwq1"]
        def late_setup():
            tt("dve", qkw[:, :], qw_bc[:, :], kw_bc[:, :], ALU.mult, r=["qw_bc", "kw_bc"], w=["qkw"])
            S.op("dve", lambda e: e.tensor_reduce(mtmp[:, 0:1], qkw[:, :], AX.X, ALU.max, apply_absolute_value=True),
                 r=["qkw"], w=["mt0"])
            S.op("dve", lambda e: e.tensor_reduce(mtmp[:, 1:2], sinks_bc[:, :], AX.X, ALU.max), r=["sinks_bc"], w=["mt1"])
            stt(mtmp[:, 2:3], mtmp[:, 0:1], 8.0, mtmp[:, 1:2], ALU.mult, ALU.max, r=["mt0", "mt1"], w=["mt2"])
            ts("dve", negM[:, :], mtmp[:, 2:3], -1.0, None, ALU.mult, ALU.bypass, r=["mt2"], w=["negM"])
            for g_ in range(2):
                ts("dve", kwq_bc[:, g_ * 64:(g_ + 1) * 64], qkw[:, :], 0.125, None, ALU.mult, ALU.bypass,
                   r=["qkw"], w=[f"kwq{g_}"])
            KWQ = ["kwq0", "kwq1"]
            act(esink[:, :], sinks_bc[:, :], AF.Exp, r=["sinks_bc", "negM"], w=["esink"], bias=negM[:, 0:1], scale=1.0)

            tt("dve", rhsB[:, :, :], ohb[:, :].unsqueeze(1).to_broadcast([32, 8, 256]),
               rb32[:, :].unsqueeze(2).to_broadcast([32, 8, 256]), ALU.mult, r=["ohb", "rb32"], w=["rhsB"])
            for q in range(4):
                pb, pk = next_pf()
                mm(pb[:, :], ones32[:, :], rhsB[:, 2 * q:2 * q + 2, :].rearrange("p a b -> p (a b)"), True, True,
                   r=["ones32", "rhsB"], w=[pk])
                act(W_sb[:, 2 * q:2 * q + 2, :].rearrange("p a b -> p (a b)"), pb[:, :], AF.Copy, r=[pk], w=[f"W_sb{q}"])
            dst = bass.AP(scr_d.tensor, scr_d.offset, [[257, 128], [SCR_H, 8], [1, 256]])
            dma("sp", dst, W_sb[:, :, :], r=[f"W_sb{q}" for q in range(4)], w=["scr"], key="scrw")
            scr_w_key = "scr"
            Tfull = EBp
            src = bass.AP(scr_d.tensor, scr_d.offset + 128, [[256, 128], [SCR_H, 8], [1, 128]])
            dma("sp", Tfull[:, :, :], src, r=[scr_w_key], w=["EBp"], key="scrr")
            tt("dve", EBc[:, :, :], Tfull[:, :, :], maskc[:, :].unsqueeze(1).to_broadcast([128, 8, 128]), ALU.mult,
               r=["EBp", "maskc"], w=["EBc"])
            tt("pool", EBp[:, :, :], Tfull[:, :, :], maskp[:, :].unsqueeze(1).to_broadcast([128, 8, 128]), ALU.mult,
               r=["EBp", "maskp"], w=["EBp"])
            ts("dve", negc[:, :], maskc[:, :], 30000.0, -30000.0, ALU.mult, ALU.add, r=["maskc"], w=["negc"])
            ts("dve", negp[:, :], maskp[:, :], 30000.0, -30000.0, ALU.mult, ALU.add, r=["maskp"], w=["negp"])
            tt("dve", EBc[:, :, :], EBc[:, :, :], negc[:, :].unsqueeze(1).to_broadcast([128, 8, 128]), ALU.add,
               r=["EBc", "negc"], w=["EBc"])
            tt("pool", EBp[:, :, :], EBp[:, :, :], negp[:, :].unsqueeze(1).to_broadcast([128, 8, 128]), ALU.add,
               r=["EBp", "negp"], w=["EBp"])
            if "m" in BM:
                act(EMc[:, :, :].rearrange("p h i -> p (h i)"), EBc[:, :, :].rearrange("p h i -> p (h i)"), AF.Exp,
                    r=["EBc"], w=["EMc"])
                act(EMp[:, :, :].rearrange("p h i -> p (h i)"), EBp[:, :, :].rearrange("p h i -> p (h i)"), AF.Exp,
                    r=["EBp"], w=["EMp"])


        def front_gen(t):
            b2 = t % 2
            xk = f"xs{t % NXS}"
            r3 = t % 3
            ssk = f"ss{b2}"
            act(xn[b2][:, :], xs[t % NXS][:, :], AF.Square, r=[xk], w=[f"xn{b2}", ssk], accum=ss[b2][:, 0:1])
            rsqrt_chain(ss[b2][:, 0:1], ss[b2][:, 1:2], ss[b2][:, 2:3], 1.0 / D_MODEL, eps6, ssk)
            if NWFOLD:
                tt("pool", xw[:, :], xs[t % NXS][:, :], nw_bc[:, :], ALU.mult, r=[xk, "nw_bc"], w=["xw"])
                ts("pool", xn[b2][:, :], xw[:, :], ss[b2][:, 2:3], 1.0, ALU.mult, ALU.mult,
                   r=["xw", ssk], w=[f"xn{b2}"])
            else:
                ts("pool", xn[b2][:, :], xs[t % NXS][:, :], ss[b2][:, 2:3], 1.0, ALU.mult, ALU.mult,
                   r=[xk, ssk], w=[f"xn{b2}"])
            yield
            tb, tk = next_tp()
            for c in range(8):
                tr(tb[:, c * 128:(c + 1) * 128], xn[b2][:, c * 128:(c + 1) * 128], r=[f"xn{b2}"], w=[tk])
            for hh_ in range(XSPLIT):
                cw = 8 // XSPLIT
                src = tb[:, hh_ * cw * 128:(hh_ + 1) * cw * 128].rearrange("p (c t) -> p c t", c=cw)
                if NWFOLD:
                    cp("dve", xT[b2][:, hh_ * cw:(hh_ + 1) * cw, :], src, r=[tk], w=[f"xT{b2}_{hh_}"])
                else:
                    tt("dve", xT[b2][:, hh_ * cw:(hh_ + 1) * cw, :], src,
                       normw_col[:, hh_ * cw:(hh_ + 1) * cw].unsqueeze(2).to_broadcast([128, cw, 128]), ALU.mult,
                       r=[tk, "normw_col"], w=[f"xT{b2}_{hh_}"])
            yield

            def proj(piece, cls="pq"):
                c0, n = piece
                pb, pk = next_pf(cls)
                for c in range(8):
                    mm(pb[:, 0:n], xT[b2][:, c, :], w_in_bf[:, c, c0:c0 + n], c == 0, c == 7,
                       r=[f"xT{b2}_{c // (8 // XSPLIT)}", f"win{c0}"], w=[pk])
                return pb, pk

            pb, pk = proj(P_QK)
            csb = cs_t[r3][:, :].rearrange("p (h d) -> p h d", h=8).unsqueeze(2).to_broadcast([128, 8, 2, 32])
            snb = sn_t[r3][:, :].rearrange("p (h d) -> p h d", h=8).unsqueeze(2).to_broadcast([128, 8, 2, 32])
            pv4 = pb[:, :].rearrange("p (h f d) -> p h f d", h=8, f=2)
            t1v = t1[:, :].rearrange("p (h f d) -> p h f d", h=8, f=2)
            t2v = t2[:, :].rearrange("p (h f d) -> p h f d", h=8, f=2)
            qkv = qkr[b2][:, :].rearrange("p (h f d) -> p h f d", h=8, f=2)
            tt("dve", t1v, pv4, csb, ALU.mult, r=[pk, f"cs{r3}"], w=["t1"])
            tt("dve", t2v, pv4, snb, ALU.mult, r=[pk, f"sn{r3}"], w=["t2"])
            tt("pool", qkv[:, :, 0, :], t1v[:, :, 0, :], t2v[:, :, 1, :], ALU.subtract, r=["t1", "t2"], w=[f"qkr{b2}"])
            tt("pool", qkv[:, :, 1, :], t1v[:, :, 1, :], t2v[:, :, 0, :], ALU.add, r=["t1", "t2"], w=[f"qkr{b2}"])
            yield
            pb, pk = proj(P_V)
            act(vb[b2][:, :], pb[:, :], AF.Copy, r=[pk], w=[f"vb{b2}"])
            yield
            sk_ = f"ssq{b2}"
            pbq, pkq = proj(P_SQ, "pl")
            act(sqj[:, 0:512], pbq[:, :], AF.Square, r=[pkq], w=["sqj_a"])
            S.op("dve", lambda e, b2=b2: e.tensor_reduce(ssq[b2][:, 0:8], sqj[:, 0:512].rearrange("p (h d) -> p h d", h=8),
                                                        AX.X, ALU.add), r=["sqj_a"], w=[sk_], cost=0.65, lat=0.1)
            pbk, pkk = proj(P_SKV, "pl")
            act(sqj[:, 512:640], pbk[:, 0:128], AF.Square, r=[pkk], w=["sqj_b"])
            S.op("dve", lambda e, b2=b2: e.tensor_reduce(ssq[b2][:, 8:10], sqj[:, 512:640].rearrange("p (h d) -> p h d", h=2),
                                                        AX.X, ALU.add), r=["sqj_b"], w=[sk_], cost=0.2, lat=0.1)
            vx = vext[r3]
            VXL = int(os.environ.get("K_VXL", "0"))
            if not VXL:
                act(vx[:, :, 0:64], pbk[:, 128:256].rearrange("p (g d) -> p g d", g=2), AF.Copy,
                    r=[pkk], w=[f"vext{r3}"])
            rsqrt_chain(ssq[b2][:, 0:10], ssq[b2][:, 10:20], ssq[b2][:, 20:30], 1.0 / 64, eps6, sk_)
            if VXL:
                act(vx[:, :, 0:64], pbk[:, 128:256].rearrange("p (g d) -> p g d", g=2), AF.Copy,
                    r=[pkk], w=[f"vext{r3}"])
            rrq = ssq[b2][:, 20:28].rearrange("p (g h) -> p h g", g=2).unsqueeze(3).to_broadcast([128, 4, 2, 64])
            tt("dve", qn[b2][:, :].rearrange("p (h g d) -> p h g d", h=4, g=2),
               pbq[:, :].rearrange("p (g h d) -> p h g d", g=2, h=4), rrq, ALU.mult,
               r=[pkq, sk_], w=[f"qn{b2}"])
            for g in range(2):
                stt(kn[b2][:, g * 64:(g + 1) * 64], pbk[:, g * 64:(g + 1) * 64], ssq[b2][:, 28 + g:29 + g],
                    kwq_bc[:, g * 64:(g + 1) * 64], ALU.mult, ALU.mult, r=[pkk, sk_] + KWQ, w=[f"kn{b2}"])
            yield
            tb, tk = next_tp()
            for p in range(4):
                tr(tb[:, p * 128:(p + 1) * 128], qkr[b2][:, p * 128:(p + 1) * 128], r=[f"qkr{b2}"], w=[tk])
            evac(CPE[0], qkT[b2][:, :, :], tb[:, 0:512].rearrange("p (a t) -> p a t", a=4), r=[tk], w=[f"qkT{b2}"])
            yield
            GV = os.environ.get("K_GATE", "pool")

            def gate(dst, dkey, pb, pk):
                if GV == "silu":
                    act(dst, pb[:, :], AF.Silu, r=[pk], w=[dkey])
                    return
                act(dst, pb[:, :], AF.Exp, r=[pk], w=[dkey], scale=-1.0)
                act(dst, dst, AF.Ln, r=[dkey, "one1"], w=[dkey], bias=one1[:, 0:1], scale=1.0)
                act(dst, dst, AF.Exp, r=[dkey], w=[dkey], scale=-1.0)
                if GV == "dve":
                    tt("dve", dst, pb[:, :], dst, ALU.mult, r=[pk, dkey], w=[dkey])
                else:
                    act(gtmp[:, :], pb[:, :], AF.Copy, r=[pk], w=["gtmp"])
                    tt("pool", dst, dst, gtmp[:, :], ALU.mult, r=[dkey, "gtmp"], w=[dkey])

            pb, pk = proj(P_RG)
            gate(gate_r[b2][:, :], f"gate_r{b2}", pb, pk)
            tt("pool", gate_r[b2][:, :], gate_r[b2][:, :], retw_bc[:, :], ALU.mult,
               r=[f"gate_r{b2}", "retw_bc"], w=[f"gate_r{b2}"])
            yield
            pb, pk = proj(P_SG)
            gate(gate_s[b2][:, :], f"gate_s{b2}", pb, pk)
            yield
            tb, tk = next_tp()
            for hg in range(4):
                tr(tb[:, hg * 128:(hg + 1) * 128], qn[b2][:, hg * 128:(hg + 1) * 128], r=[f"qn{b2}"], w=[tk])
            tr(tb[:, 512:640], kn[b2][:, :], r=[f"kn{b2}"], w=[tk])
            evac(CPE[1], qnT[b2][:, :], tb[:, 0:512], r=[tk], w=[f"qnT{b2}"])
            evac(CPE[2], knT[r3][:, :], tb[:, 512:640], r=[tk], w=[f"knT{r3}"])
            yield

        def back_gen(t):
            b2 = t % 2
            xk = f"xs{t % NXS}"
            qT_ = qkT[b2]
            r3 = t % 3
            p3 = (t - 1) % 3
            dk = f"den{b2}"
            for which in (0, 1):
                if which == 1 and t == 0:
                    continue
                for g in range(2):
                    lo = g * 64
                    kt = knT[r3] if which == 0 else knT[p3]
                    kkey = f"knT{r3}" if which == 0 else f"knT{p3}"
                    EB = EBc if which == 0 else EBp
                    ekey = "EBc" if which == 0 else "EBp"
                    pcb, pck = next_pf("bs")
                    mm(pcb[:, :], kt[lo:lo + 64, :], qnT[b2][lo:lo + 64, :], True, True,
                       r=[kkey, f"qnT{b2}"], w=[pck])
                    i_ = 2 * g + which
                    if BM[i_] == "a":
                        tt("dve", Praw[i_][:, :], pcb[:, :], EB[:, g * 4:(g + 1) * 4, :].rearrange("p h i -> p (h i)"),
                           ALU.add, r=[pck, ekey], w=[f"Praw{i_}"])
                        act(Pbf[i_][:, :], Praw[i_][:, :], AF.Exp, r=[f"Praw{i_}", "negM"], w=[f"Pbf{i_}"],
                            bias=negM[:, 0:1], scale=1.0)
                    else:
                        EM = EMc if which == 0 else EMp
                        act(Praw[i_][:, :], pcb[:, :], AF.Exp, r=[pck, "negM"], w=[f"Praw{i_}"],
                            bias=negM[:, 0:1], scale=1.0)
                        tt("pool", Pbf[i_][:, :], Praw[i_][:, :],
                           EM[:, g * 4:(g + 1) * 4, :].rearrange("p h i -> p (h i)"), ALU.mult,
                           r=[f"Praw{i_}", "EMc" if which == 0 else "EMp"], w=[f"Pbf{i_}"])
            yield
            psbs = [next_pf("bs"), next_pf("bs")]
            for h in range(4):
                lo = (h % 2) * 64
                psb, psk = psbs[h % 2]
                mm(psb[:, (h // 2) * 128:(h // 2 + 1) * 128], qT_[lo:lo + 64, 2 + h // 2, :], qT_[lo:lo + 64, h // 2, :],
                   True, True, r=[f"qkT{b2}"], w=[psk])
            sT4 = sT[b2][:, :].rearrange("p (a hh i) -> p a hh i", a=2, hh=2)
            for hh in range(2):
                psb, psk = psbs[hh]
                tt("dve", sT4[:, :, hh, :], psb[:, 0:256].rearrange("p (a i) -> p a i", a=2),
                   maskc[:, :].unsqueeze(1).to_broadcast([128, 2, 128]), ALU.mult, r=[psk, "maskc"], w=[f"sT{b2}"])
            yield
            pkvb, pkvk = next_pf("bl")
            for p in range(2):
                mm(pkvb[:, p * 256:(p + 1) * 256], qkr[b2][:, 256 + p * 128:256 + (p + 1) * 128],
                   vb[b2][:, p * 256:(p + 1) * 256], True, True, r=[f"qkr{b2}", f"vb{b2}"], w=[pkvk])
            prb, prk = next_pf("bl")
            tbk = f"Tb{b2}"
            for p in range(2):
                if t > 0:
                    mm(prb[:, p * 256:(p + 1) * 256], qT_[:, p, :], Tb[b2][:, p, :], True, False,
                       r=[f"qkT{b2}", tbk], w=[prk])
                for hh in range(2):
                    h = 2 * p + hh
                    mm(prb[:, h * 128:(h + 1) * 128], sT[b2][:, h * 128:(h + 1) * 128], vb[b2][:, h * 128:(h + 1) * 128],
                       t == 0, (t == 0) or hh == 1, r=[f"sT{b2}", f"vb{b2}"], w=[prk])
            STL = int(os.environ.get("K_STL", "1"))
            def _state_update():
                if t + 1 < NT:
                    for p in range(2):
                        for hh in range(2):
                            lo = hh * 64
                            stt(Tst[lo:lo + 64, p, hh * 128:(hh + 1) * 128], Tst[lo:lo + 64, p, hh * 128:(hh + 1) * 128],
                                gC[lo:lo + 64, p:p + 1], pkvb[lo:lo + 64, p * 256 + hh * 128:p * 256 + (hh + 1) * 128],
                                ALU.mult, ALU.add, r=["Tst", "gC", pkvk], w=["Tst"])
                    nb = (t + 1) % 2
                    for p in range(2):
                        ts("pool", Tb[nb][:, p, :], Tst[:, p, :], gC[:, p:p + 1], 1.0, ALU.mult, ALU.mult,
                           r=["Tst", "gC"], w=[f"Tb{nb}"])
            if not STL:
                _state_update()
            mvk = f"bmv{b2}"
            for h in range(4):
                S.op("dve", lambda e, h=h, prb=prb: e.bn_stats(bst[:, h, :], R(prb[:, h * 128:(h + 1) * 128])),
                     r=[prk], w=["bst"], cost=0.25, lat=0.1)
            for h in range(4):
                S.op("dve", lambda e, h=h, b2=b2: e.bn_aggr(bmv[b2][:, 2 * h:2 * h + 2], bst[:, h, :]),
                     r=["bst"], w=[mvk, f"bmvm{b2}"], cost=0.1, lat=0.1)
            mv3 = bmv[b2][:, 0:8].rearrange("p (h s) -> p h s", s=2)
            act(bmv[b2][:, 8:12], mv3[:, :, 1], AF.Ln, r=[mvk, "eps"], w=[mvk], bias=eps5[:, 0:1], scale=1.0)
            act(bmv[b2][:, 12:16], bmv[b2][:, 8:12], AF.Exp, r=[mvk], w=[mvk], scale=-0.5)
            if B2S:
                for h in range(4):
                    stt(ybuf[:, h * 128:(h + 1) * 128], prb[:, h * 128:(h + 1) * 128], bmv[b2][:, 2 * h:2 * h + 1],
                        gate_r[b2][:, h * 128:(h + 1) * 128], ALU.subtract, ALU.mult,
                        r=[prk, f"bmvm{b2}", f"gate_r{b2}"], w=[f"ybuf{h}"])
                for h in range(4):
                    ts(B2E, mixed[b2][:, h * 128:(h + 1) * 128], ybuf[:, h * 128:(h + 1) * 128],
                       bmv[b2][:, 12 + h:13 + h], 1.0, ALU.mult, ALU.mult, r=[f"ybuf{h}", mvk], w=[f"mixed{b2}"])
            else:
                for h in range(4):
                    ts("dve", ybuf[:, h * 128:(h + 1) * 128], prb[:, h * 128:(h + 1) * 128], bmv[b2][:, 2 * h:2 * h + 1],
                       bmv[b2][:, 12 + h:13 + h], ALU.subtract, ALU.mult, r=[prk, mvk], w=["ybuf"])
                tt("pool", mixed[b2][:, 0:512], ybuf[:, :], gate_r[b2][:, :], ALU.mult,
                   r=["ybuf", f"gate_r{b2}"], w=[f"mixed{b2}"])
            if STL:
                _state_update()
            yield
            for g in range(2):
                i_c, i_p = 2 * g, 2 * g + 1
                pob, pok = next_pf("bl")
                po3 = pob[:, 0:260].rearrange("p (h e) -> p h e", h=4)
                for hg in range(4):
                    mm(po3[:, hg, :], Pbf[i_c][:, hg * 128:(hg + 1) * 128], vext[r3][:, g, :], True, t == 0,
                       r=[f"Pbf{i_c}", f"vext{r3}"], w=[pok])
                    if t > 0:
                        mm(po3[:, hg, :], Pbf[i_p][:, hg * 128:(hg + 1) * 128], vext[p3][:, g, :], False, True,
                           r=[f"Pbf{i_p}", f"vext{p3}"], w=[pok])
                tt("dve", den[b2][:, g * 4:(g + 1) * 4], po3[:, :, 64], esink[:, g * 4:(g + 1) * 4], ALU.add,
                   r=[pok, "esink"], w=[dk])
                S.op("dve", lambda e, b2=b2, g=g: e.reciprocal(den[b2][:, 8 + g * 4:8 + (g + 1) * 4],
                                                               den[b2][:, g * 4:(g + 1) * 4]), r=[dk], w=[dk], cost=0.18, lat=0.1)
                if B4S:
                    tt("dve", gs2[:, g * 256:(g + 1) * 256].rearrange("p (h d) -> p h d", h=4), po3[:, :, 0:64],
                       gate_s[b2][:, g * 256:(g + 1) * 256].rearrange("p (h d) -> p h d", h=4), ALU.mult,
                       r=[pok, f"gate_s{b2}"], w=[f"gs2_{g}"])
                    tt(B4E, mixed[b2][:, 512 + g * 256:512 + (g + 1) * 256].rearrange("p (h d) -> p h d", h=4),
                       gs2[:, g * 256:(g + 1) * 256].rearrange("p (h d) -> p h d", h=4),
                       den[b2][:, 8 + g * 4:8 + (g + 1) * 4].unsqueeze(2).to_broadcast([128, 4, 64]), ALU.mult,
                       r=[f"gs2_{g}", dk], w=[f"mixed{b2}"])
                else:
                    tt(os.environ.get("K_GS2E", "dve"), gs2[:, g * 256:(g + 1) * 256].rearrange("p (h d) -> p h d", h=4),
                       gate_s[b2][:, g * 256:(g + 1) * 256].rearrange("p (h d) -> p h d", h=4),
                       den[b2][:, 8 + g * 4:8 + (g + 1) * 4].unsqueeze(2).to_broadcast([128, 4, 64]), ALU.mult,
                       r=[f"gate_s{b2}", dk], w=["gs2"])
                    tt("dve", mixed[b2][:, 512 + g * 256:512 + (g + 1) * 256].rearrange("p (h d) -> p h d", h=4),
                       po3[:, :, 0:64], gs2[:, g * 256:(g + 1) * 256].rearrange("p (h d) -> p h d", h=4), ALU.mult,
                       r=[pok, "gs2"], w=[f"mixed{b2}"])
            yield
            tb, tk = next_tp("tB")
            for c in range(8):
                tr(tb[:, c * 128:(c + 1) * 128], mixed[b2][:, c * 128:(c + 1) * 128], r=[f"mixed{b2}"], w=[tk])
            for hh_ in range(MSPLIT):
                cw = 8 // MSPLIT
                evac(CPE[3], mT[:, hh_ * cw:(hh_ + 1) * cw, :],
                     tb[:, hh_ * cw * 128:(hh_ + 1) * cw * 128].rearrange("p (c t) -> p c t", c=cw),
                     r=[tk], w=[f"mT{hh_}"])
            yield
            for hf in range(2):
                pb, pk = next_pf("bo")
                for c in range(8):
                    mm(pb[:, :], mT[:, c, :], w_out_bf[:, c, hf * 512:(hf + 1) * 512], c == 0, c == 7,
                       r=[f"mT{c // (8 // MSPLIT)}", f"wout{hf}"], w=[pk])
                tt("dve", ot[b2][:, hf * 512:(hf + 1) * 512], pb[:, :], xs[t % NXS][:, hf * 512:(hf + 1) * 512], ALU.add,
                   r=[pk, xk], w=[f"ot{b2}_{hf}"])
                if OSPLIT:
                    dma("sp", out_d[t * 128:(t + 1) * 128, hf * 512:(hf + 1) * 512], ot[b2][:, hf * 512:(hf + 1) * 512],
                        r=[f"ot{b2}_{hf}"], w=[f"out{t}_{hf}"], key=f"ot{b2}_{hf}")
            if not OSPLIT:
                dma("sp", out_d[t * 128:(t + 1) * 128, :], ot[b2][:, :], r=[f"ot{b2}_0", f"ot{b2}_1"],
                    w=[f"out{t}_0", f"out{t}_1"], key=f"ot{b2}")
            yield

        F = {t: front_gen(t) for t in range(NT)}
        B = {t: back_gen(t) for t in range(NT)}

        def step(g):
            next(g, None)

        if NT > 2:
            load_tile(2)
        if NT > 1:
            load_rot(1)
        late_setup()
        step(F[0])
        if NT > 1:
            step(F[1])
        for _ in range(8):
            step(F[0])
        ORDER = "FBBFBFBFFFBFFB"
        for t in range(NT):
            if t + 3 < NT:
                load_tile(t + 3)
            if t + 2 < NT:
                load_rot(t + 2)
                step(F[t + 2])
            for o in ORDER:
                if o == "B":
                    step(B[t])
                elif t + 1 < NT:
                    step(F[t + 1])


        fin_r = [f"out{t}_{hf}" for t in range(NT) for hf in range(2)]
        S.op("sp", None, r=fin_r, w=())
        S.emit(nc, es)
    return nc


_CONSTS = None


def kernel(x, norm_w, w_in, ret_norm_w, q_norm_w, k_norm_w, sinks, rel_bias, w_out, _dbg_tile=None):
    global _CONSTS
    if _CONSTS is None:
        _CONSTS = _consts()
    f = lambda a: np.ascontiguousarray(np.asarray(a, dtype=np.float32))
    x = f(x)
    shared = {
        "norm_w": f(norm_w), "w_in": f(w_in), "ret_norm_w": f(ret_norm_w), "q_norm_w": f(q_norm_w),
        "k_norm_w": f(k_norm_w), "sinks": f(sinks), "rel_bias": f(rel_bias), "w_out": f(w_out),
    }
    shared.update(_CONSTS)
    nc = build_nc(_dbg_tile)
    in_maps = [dict(shared, x=np.ascontiguousarray(x[i])) for i in range(N_CORES)]
    res = run_bass_kernel_spmd(nc, in_maps, core_ids=list(range(N_CORES)))
    out = np.stack([np.asarray(r["out"], dtype=np.float32).reshape(SEQ, D_MODEL) for r in res.results], axis=0)
    if _dbg_tile is not None:
        return out, res.results
    return out
```
